# Optimizing a Trainium2 kernel written in Bass

```python
import jax, jax.numpy as jnp
from jax import lax
import numpy as np

D_MODEL = 1024
BATCH = 4
SEQ = 4096
DEPTH = 1

CHUNK = 64
MIX_WIDTH = D_MODEL
HGRN_WIDTH = MIX_WIDTH // 2
CONV_WIDTH = MIX_WIDTH - HGRN_WIDTH
HGRN_HEAD_DIM = 128
HGRN_HEADS = HGRN_WIDTH // HGRN_HEAD_DIM
CONV_K = 3
D_FF = 4 * D_MODEL
ALPHA = (2 * DEPTH) ** 0.25
BETA = (8 * DEPTH) ** -0.25
EPS = 1e-5
IN_COLS = 4 * HGRN_WIDTH + 3 * CONV_WIDTH
SPLITS = (HGRN_WIDTH, 2 * HGRN_WIDTH, 3 * HGRN_WIDTH, 4 * HGRN_WIDTH,
          4 * HGRN_WIDTH + CONV_WIDTH, 4 * HGRN_WIDTH + 2 * CONV_WIDTH)

kernel_name = "hybrid_hgrn2_shortconv_deepnorm_layer"


def layer_norm(x, g, b):
    xf = x.astype(jnp.float32)
    mu = jnp.mean(xf, axis=-1, keepdims=True)
    xc = xf - mu
    var = jnp.mean(jnp.square(xc), axis=-1, keepdims=True)
    y = xc * lax.rsqrt(var + EPS) * g.astype(jnp.float32) + b.astype(jnp.float32)
    return y.astype(x.dtype)


def hgrn2_chunkwise(q, k, v, g):
    bsz, seq, n_h, d_k = q.shape
    d_v = v.shape[-1]
    n_c = seq // CHUNK

    def to_chunks(a):
        return a.reshape(bsz, n_c, CHUNK, n_h, a.shape[-1]).transpose(1, 0, 3, 2, 4)

    q, k, v, g = to_chunks(q), to_chunks(k), to_chunks(v), to_chunks(g)
    b = jnp.cumsum(g, axis=-2)
    b_ref = b[..., CHUNK // 2:CHUNK // 2 + 1, :]
    b_last = b[..., -1:, :]
    causal = jnp.tril(jnp.ones((CHUNK, CHUNK), dtype=bool))
    scores = jnp.einsum('nbhck,nbhsk->nbhcs', q * jnp.exp(b - b_ref), k * jnp.exp(b_ref - b))
    scores = jnp.where(causal, scores, 0.0)
    o_intra = jnp.einsum('nbhcs,nbhsv->nbhcv', scores, v)
    q_inter = q * jnp.exp(b)
    k_state = k * jnp.exp(b_last - b)
    chunk_decay = jnp.exp(b_last[..., 0, :])

    def step(state, inp):
        q_c, k_c, v_c, d_c = inp
        o_c = jnp.einsum('bhck,bhkv->bhcv', q_c, state)
        state = d_c[..., None] * state + jnp.einsum('bhck,bhcv->bhkv', k_c, v_c)
        return state, o_c

    s0 = jnp.zeros((bsz, n_h, d_k, d_v), q.dtype)
    _, o_inter = lax.scan(step, s0, (q_inter, k_state, v, chunk_decay))
    o = o_intra + o_inter
    return o.transpose(1, 0, 3, 2, 4).reshape(bsz, seq, n_h, d_v)


def causal_depthwise_conv(z, w):
    rhs = w.astype(z.dtype).reshape(CONV_K, 1, z.shape[-1])
    return lax.conv_general_dilated(
        z, rhs, window_strides=(1,), padding=[(CONV_K - 1, 0)],
        dimension_numbers=('NWC', 'WIO', 'NWC'), feature_group_count=z.shape[-1])


def token_mixer(h, w_in, lower_bound, gate_norm_w, conv_w, w_out):
    bsz, seq, _ = h.shape
    proj = h @ w_in
    q, f_pre, i_in, o_gate, b_gate, c_gate, u = jnp.split(proj, SPLITS, axis=-1)

    f = lower_bound + (1.0 - lower_bound) * jax.nn.sigmoid(f_pre.astype(jnp.float32))
    log_f = jnp.log(f)
    k = 1.0 - f
    heads = lambda a: a.reshape(bsz, seq, HGRN_HEADS, HGRN_HEAD_DIM)
    o = hgrn2_chunkwise(heads(q.astype(jnp.float32)), heads(k),
                        heads(i_in.astype(jnp.float32)), heads(log_f))
    o = o * lax.rsqrt(jnp.mean(jnp.square(o), axis=-1, keepdims=True) + EPS)
    o = o.reshape(bsz, seq, HGRN_WIDTH) * gate_norm_w.astype(jnp.float32) \
        * jax.nn.silu(o_gate.astype(jnp.float32))
    o = o.astype(h.dtype)

    y = b_gate * causal_depthwise_conv(c_gate * u, conv_w)

    return jnp.concatenate([o, y], axis=-1) @ w_out


def squared_relu_mlp(h, w1, w2):
    return jnp.square(jax.nn.relu(h @ w1)) @ w2


def setup_inputs(seed: int = 0) -> dict:
    key = jax.random.key(seed)
    ks = jax.random.split(key, 12)
    nrm = lambda k, shape: jax.random.normal(k, shape, jnp.float32)
    return {
        "x": nrm(ks[0], (BATCH, SEQ, D_MODEL)),
        "w_in": nrm(ks[1], (DEPTH, D_MODEL, IN_COLS)) * D_MODEL ** -0.5,
        "lb_logits": 0.1 * nrm(ks[2], (DEPTH + 1, HGRN_WIDTH)),
        "gate_norm_w": 1.0 + 0.02 * nrm(ks[3], (DEPTH, HGRN_WIDTH)),
        "conv_w": nrm(ks[4], (DEPTH, CONV_K, CONV_WIDTH)) * CONV_K ** -0.5,
        "w_out": nrm(ks[5], (DEPTH, MIX_WIDTH, D_MODEL)) * (MIX_WIDTH ** -0.5 * BETA),
        "ln1_g": 1.0 + 0.02 * nrm(ks[6], (DEPTH, D_MODEL)),
        "ln1_b": 0.02 * nrm(ks[7], (DEPTH, D_MODEL)),
        "w_ff1": nrm(ks[8], (DEPTH, D_MODEL, D_FF)) * D_MODEL ** -0.5,
        "w_ff2": nrm(ks[9], (DEPTH, D_FF, D_MODEL)) * (D_FF ** -0.5 * BETA),
        "ln2_g": 1.0 + 0.02 * nrm(ks[10], (DEPTH, D_MODEL)),
        "ln2_b": 0.02 * nrm(ks[11], (DEPTH, D_MODEL)),
    }


def reference(x, w_in, lb_logits, gate_norm_w, conv_w, w_out, ln1_g, ln1_b,
              w_ff1, w_ff2, ln2_g, ln2_b):
    lower_bounds = jnp.cumsum(jax.nn.softmax(lb_logits.astype(jnp.float32), axis=0), axis=0)
    h = x
    for l in range(DEPTH):
        mix = token_mixer(h, w_in[l], lower_bounds[l], gate_norm_w[l], conv_w[l], w_out[l])
        h = layer_norm(ALPHA * h + mix, ln1_g[l], ln1_b[l])
        h = layer_norm(ALPHA * h + squared_relu_mlp(h, w_ff1[l], w_ff2[l]), ln2_g[l], ln2_b[l])
    return h
```

```python
import numpy as np
from contextlib import ExitStack
import concourse.bass as bass
import concourse.mybir as mybir
from concourse.bass_utils import run_bass_kernel_spmd

F32 = mybir.dt.float32
F32R = mybir.dt.float32r
BF16 = mybir.dt.bfloat16
AF = mybir.ActivationFunctionType
ALU = mybir.AluOpType

N_CORES = 8
D = 1024
KC = 8
T = 2048
TB = 512
NB1 = T // TB
T2 = 256
NB2 = T // T2
H = 4
DFF = 4096
NF = DFF // 128
ALPHA = 2.0 ** 0.25
EPS = 1e-5
CQ, CF, CI, COG, CB, CC, CU = 0, 512, 1024, 1536, 2048, 2560, 3072
PL0, PL1, PGN, PCW, PG1, PB1, PG2, PB2, NPRM = 0, 4, 8, 12, 24, 32, 40, 48, 56

SAME_ENGINE_SYNC = True


class Res:
    __slots__ = ("name", "w", "r", "dsem", "dcnt", "wtick", "rtick")

    def __init__(self, name):
        self.name = name
        self.w = None
        self.r = {}
        self.dsem = None
        self.dcnt = 0
        self.wtick = -1
        self.rtick = -1


class Sched:
    def __init__(self, nc, es):
        self.nc = nc
        self.es = es
        self.engs = {"pe": nc.tensor, "act": nc.scalar, "dve": nc.vector, "pool": nc.gpsimd, "sp": nc.sync}
        self.sem = {k: es.enter_context(nc.semaphore("s_" + k)) for k in self.engs}
        self.cnt = {k: 0 for k in self.engs}
        self.seen = {k: {} for k in self.engs}
        self.semobj = {}
        self.nsem = 0
        self.store_res = []
        self.tick = 0

    def _semof(self, key):
        if isinstance(key, str) and key in self.sem:
            return self.sem[key]
        return self.semobj[key]

    def _wait(self, eng, deps):
        best = {}
        for key, val in deps:
            if key == eng and (eng == "pe" or not SAME_ENGINE_SYNC):
                continue
            if val > best.get(key, 0):
                best[key] = val
        for key, val in best.items():
            if key in self.cnt:
                assert val <= self.cnt[key], "wait on not-yet-signalled %s seq %d (cnt %d)" % (key, val, self.cnt[key])
            if val > self.seen[eng].get(key, 0):
                self.engs[eng].wait_ge(self._semof(key), val)
                self.seen[eng][key] = val

    def _deps(self, reads, writes):
        deps = []
        for r in reads:
            if r.w is not None:
                deps.append(r.w)
        for w in writes:
            if w.w is not None:
                deps.append(w.w)
            deps.extend(w.r.items())
        return deps

    def op(self, eng, fn, reads=(), writes=(), signal=True):
        self._wait(eng, self._deps(reads, writes))
        ins = fn(self.engs[eng])
        self.tick += 1
        seq = self.cnt[eng] + 1
        if signal:
            ins.then_inc(self.sem[eng], 1)
            self.cnt[eng] = seq
        for r in reads:
            if r not in writes:
                r.r[eng] = max(r.r.get(eng, 0), seq)
                r.rtick = self.tick
        for w in writes:
            w.w = (eng, seq)
            w.r = {}
            w.wtick = self.tick
            w.rtick = -1
        return ins

    def _dsem(self, res):
        if res.dsem is None:
            self.nsem += 1
            key = "d%d_%s" % (self.nsem, res.name)
            self.semobj[key] = self.es.enter_context(self.nc.semaphore(key))
            res.dsem = key
        return res.dsem

    def load(self, q, out, in_, res):
        self._wait(q, self._deps((), (res,)))
        key = self._dsem(res)
        res.dcnt += 16
        self.engs[q].dma_start(out=out, in_=in_).then_inc(self.semobj[key], 16)
        res.w = (key, res.dcnt)
        res.r = {}

    def store(self, q, out, in_, res):
        self._wait(q, self._deps((res,), ()))
        key = self._dsem(res)
        res.dcnt += 16
        self.engs[q].dma_start(out=out, in_=in_).then_inc(self.semobj[key], 16)
        res.r[key] = res.dcnt
        if res not in self.store_res:
            self.store_res.append(res)

    def finish(self, q="sp"):
        for res in self.store_res:
            self.engs[q].wait_ge(self.semobj[res.dsem], res.dcnt)


def build_program():
    nc = bass.Bass("TRN2", target_bir_lowering=False)
    xT_d = nc.dram_tensor("xT", [D, T + 2], F32, kind="ExternalInput").ap()
    xb_d = nc.dram_tensor("xb", [NB1, 128, KC * TB], F32, kind="ExternalInput").ap()
    xpb_d = nc.dram_tensor("xpb", [NB1, 128, KC * TB], F32, kind="ExternalInput").ap()
    win_d = nc.dram_tensor("w_in", [7, 128, KC * 512], F32, kind="ExternalInput").ap()
    wout_d = nc.dram_tensor("w_out", [KC, 128, KC * 128], F32, kind="ExternalInput").ap()
    wff1_d = nc.dram_tensor("w_ff1", [8, 128, KC * 512], F32, kind="ExternalInput").ap()
    wff2_d = nc.dram_tensor("w_ff2", [8, 128, 4 * D], F32, kind="ExternalInput").ap()
    prm_d = nc.dram_tensor("prm", [128, NPRM], F32, kind="ExternalInput").ap()
    cst_d = nc.dram_tensor("cst", [128, 640], F32, kind="ExternalInput").ap()
    outT_d = nc.dram_tensor("outT", [D, T], F32, kind="ExternalOutput").ap()

    xT_v = xT_d.rearrange("(kc p) t -> p kc t", p=128)
    k3 = lambda ap: ap.rearrange("p (k c) -> p k c", k=KC)
    outT_v = outT_d.rearrange("(j p) t -> p j t", p=128)

    with ExitStack() as es:
        S = Sched(nc, es)

        def sb(name, shape, dt, stack=es):
            return stack.enter_context(nc.sbuf_tensor("sb_" + name, shape, dt))

        def ps(name, shape, dt, stack=es):
            return stack.enter_context(nc.psum_tensor("ps_" + name, shape, dt))

        def act(out, in_, func, reads, writes, scale=1.0, bias=0.0):
            return S.op("act", lambda e: e.activation(out=out, in_=in_, func=func, scale=scale, bias=bias),
                        reads, writes)

        def tt(out, in0, in1, op, reads, writes, eng="dve"):
            return S.op(eng, lambda e: e.tensor_tensor(out=out, in0=in0, in1=in1, op=op), reads, writes)

        def ts(out, in0, s1, s2, op0, op1, reads, writes, eng="dve"):
            return S.op(eng, lambda e: e.tensor_scalar(out=out, in0=in0, scalar1=s1, scalar2=s2, op0=op0, op1=op1),
                        reads, writes)

        def stt(out, in0, scalar, in1, op0, op1, reads, writes, eng="dve"):
            return S.op(eng, lambda e: e.scalar_tensor_tensor(out=out, in0=in0, scalar=scalar, in1=in1,
                                                              op0=op0, op1=op1), reads, writes)

        def cp(out, in_, reads, writes, eng="dve"):
            return S.op(eng, lambda e: e.tensor_copy(out=out, in_=in_), reads, writes)

        def mm(out, lhsT, rhs, start, stop, reads, writes, signal, **kw):
            return S.op("pe", lambda e: e.matmul(out, lhsT=lhsT, rhs=rhs, start=start, stop=stop, **kw),
                        reads, writes, signal=signal)

        prm = sb("prm", [128, NPRM], F32)
        r_prm = Res("prm")
        ones_b = sb("ones_b", [128, 128], BF16)
        r_ones = Res("ones")
        S.op("dve", lambda e: e.memset(ones_b[:], 1.0), [], [r_ones])
        S.load("sp", prm[:], prm_d[:, :], r_prm)
        lbt = sb("lbt", [128, 12], F32)
        r_lb = Res("lb")
        tt(lbt[:, 8:12], prm[:, PL0:PL0 + 4], prm[:, PL1:PL1 + 4], ALU.subtract, [r_prm], [r_lb])
        act(lbt[:, 0:4], lbt[:, 8:12], AF.Exp, [r_lb], [r_lb], scale=-1.0)
        act(lbt[:, 0:4], lbt[:, 0:4], AF.Ln, [r_lb], [r_lb], scale=1.0, bias=1.0)
        act(lbt[:, 0:4], lbt[:, 0:4], AF.Exp, [r_lb], [r_lb], scale=-1.0)
        ts(lbt[:, 4:8], lbt[:, 0:4], -1.0, 1.0, ALU.mult, ALU.add, [r_lb], [r_lb])

        slab = [sb("slab%d" % i, [128, 4096], BF16) for i in range(16)]
        mixT = sb("mixT", [128, KC, T], BF16)
        r_mix = [[Res("mix_%d_%d" % (c, b)) for b in range(NB1)] for c in range(KC)]

        reg = [slab[r][:, :].rearrange("p (k c) -> p k c", k=KC) for r in range(8)]
        r_wff1 = [Res("wff1_%d" % i) for i in range(8)]
        W2SLAB = {0: 9, 1: 10, 2: 11, 3: 12, 4: 13, 5: 8, 6: 14, 7: 15}
        wff2c = [slab[W2SLAB[i]][:, :].rearrange("p (f c) -> p f c", f=4) for i in range(8)]
        r_wff2 = [Res("wff2_%d" % i) for i in range(8)]

        def f32view(si, j):
            return slab[si][:, 1024 * j:1024 * (j + 1)].bitcast(F32)
        with ExitStack() as p1:
            cst = sb("cst", [128, 640], F32, p1)
            r_cst = Res("cst")
            S.load("sp", cst[:], cst_d[:, :], r_cst)
            ident = sb("ident", [128, 128], BF16, p1)
            r_ident = Res("ident")
            cp(ident[:], cst[:, 0:128], [r_cst], [r_ident])
            mask4 = cst[:, 128:640]
            ones_f = sb("ones_f", [128, 128], F32, p1)
            r_onesf = Res("ones_f")
            S.op("dve", lambda e: e.memset(ones_f[:], 1.0), [], [r_onesf])
            win = reg[0:7]
            r_win = [Res("win%d" % i) for i in range(7)]
            r_win1h = [Res("win1_%d" % h) for h in range(H)]
            NXS = 3
            xs = [reg[7], slab[8][:, :].rearrange("p (k t) -> p k t", k=KC),
                  slab[9][:, :].rearrange("p (k t) -> p k t", k=KC)]
            r_xs = [Res("xs%d" % i) for i in range(NXS)]
            xh = sb("xh", [128, KC, 2], BF16, p1)
            r_xh = Res("xh")
            ones512 = sb("ones512", [128, TB], F32, p1)
            r_o512 = Res("ones512")
            S.op("dve", lambda e: e.memset(ones512[:], 1.0), [], [r_o512])
            NS = 2
            _sv = [f32view(10 + i // 4, i % 4) for i in range(12)]
            r_sv = [Res("sv%d" % i) for i in range(12)]

            class TSet:
                def __init__(self, own, h):
                    if own:
                        st = h % 2
                        ix = dict(kr=st, gg=2 + st, qq=4 + st, bb=6 + st, eq=8 + st, ek=10 + st)
                    else:
                        st = h % 3
                        ix = dict(kr=4 * st, gg=4 * st + 1, bb=4 * st + 2, ek=4 * st + 3)
                    for k_, i_ in ix.items():
                        setattr(self, k_, _sv[i_])
                        setattr(self, "r_" + k_, r_sv[i_])
            Qt = [[slab[13 + p][:, 512 * h:512 * (h + 1)] for h in range(H)] for p in range(2)]
            Kt = [[slab[13 + p][:, 2048 + 512 * h:2048 + 512 * (h + 1)] for h in range(H)] for p in range(2)]
            sm = [[sb("sm%d_%d" % (p, h), [128, 16], F32, p1) for h in range(H)] for p in range(2)]
            r_Qt = [[Res("Qt%d_%d" % (p, h)) for h in range(H)] for p in range(2)]
            r_Kt = [[Res("Kt%d_%d" % (p, h)) for h in range(H)] for p in range(2)]
            r_sm = [[Res("sm%d_%d" % (p, h)) for h in range(H)] for p in range(2)]
            for p in range(2):
                for h in range(H):
                    S.op("dve", lambda e, p=p, h=h: e.memset(sm[p][h][:], 0.0), [], [r_sm[p][h]])
            vtok = [slab[15][:, 2048 * p:2048 * (p + 1)].rearrange("p (g c) -> p g c", g=4) for p in range(2)]
            r_vtok = [[Res("vtok%d_%d" % (p, g)) for g in range(4)] for p in range(2)]
            sil2 = [[sb("sil%d_%d" % (q, h), [128, TB], F32, p1) for h in range(H)] for q in range(2)]
            r_sil2 = [[Res("sil%d_%d" % (q, h)) for h in range(H)] for q in range(2)]
            Ktok = [sb("Ktok%d" % i, [128, 512], BF16, p1) for i in range(2)]
            r_Ktok = [Res("Ktok%d" % i) for i in range(2)]
            scm = [sb("scm%d" % i, [128, 512], BF16, p1) for i in range(2)]
            r_scm = [Res("scm%d" % i) for i in range(2)]
            Sf = sb("Sf", [128, 512], F32, p1)
            Uf = sb("Uf", [128, 512], F32, p1)
            Sb = sb("Sb", [128, 512], BF16, p1)
            r_Sf, r_Uf, r_Sb = Res("Sf"), Res("Uf"), Res("Sb")
            S.op("dve", lambda e: e.memset(Sf[:], 0.0), [], [r_Sf])
            S.op("dve", lambda e: e.memset(Sb[:], 0.0), [], [r_Sb])
            og_ = sb("og", [128, 512], F32, p1)
            o2_ = sb("o2", [128, 512], BF16, p1)
            rs_ = sb("rs", [128, 512], F32, p1)
            r_og, r_o2, r_rs = Res("og"), Res("o2"), Res("rs")
            u_sb = sb("u_sb", [128, TB], F32, p1)
            z_ext = [sb("z_ext%d" % i, [128, TB + 2], F32, p1) for i in range(2)]
            acc = [sb("acc%d" % i, [128, TB], F32, p1) for i in range(2)]
            zc = sb("zc", [128, 4, 2], F32, p1)
            uh = sb("uh", [128, 2], F32, p1)
            r_u, r_uh = Res("u"), Res("uh")
            r_z = [Res("z0"), Res("z1")]
            r_acc = [Res("acc0"), Res("acc1")]
            r_zc = [Res("zc%d" % c) for c in range(4)]
            NPJ = 4
            PJ = [ps("PJ%d" % i, [128, 512], F32, p1) for i in range(NPJ)]
            r_PJ = [Res("PJ%d" % i) for i in range(NPJ)]
            TR = ps("TR", [128, 1024], BF16, p1)
            r_TR = Res("TR")
            SC = ps("SC", [128, 512], F32, p1)
            r_SC = Res("SC")
            SQ, r_SQ = SC, r_SC
            OO = ps("OO", [128, 512], F32, p1)
            r_OO = Res("OO")
            ST = ps("ST", [128, 512], F32, p1)
            r_ST = Res("ST")
            pj_i = [0]
            pj_n = [NPJ]

            def pj_alloc():
                free = [i for i in range(pj_n[0]) if r_PJ[i].w is None or r_PJ[i].rtick >= 0]
                assert free, "all in-proj PSUM banks hold data whose consumer has not been emitted yet"
                return min(free, key=lambda i: (r_PJ[i].rtick, r_PJ[i].wtick))

            def proj_fm(xsrc, r_x, col0, ncol=TB):
                i = pj_alloc()
                ch, off = col0 // 512, col0 % 512
                rw = r_win1h[off // 128] if ch == 1 else r_win[ch]
                for kc in range(KC):
                    mm(PJ[i][:, 0:ncol], win[ch][:, kc, off:off + 128], xsrc[:, kc, 0:ncol],
                       kc == 0, kc == KC - 1, [rw, r_x], [r_PJ[i]], kc == KC - 1)
                return PJ[i], r_PJ[i]

            seq = [(False, b) for b in range(NB1)] + [(True, b) for b in range(NB1)]
            NBLK = len(seq)

            def issue_x(n):
                own, b = seq[n]
                slot = n % NXS
                src = k3(xb_d[b]) if own else k3(xpb_d[b])
                S.load("pool", xs[slot][:], src, r_xs[slot])

            def gate_head(n, h):
                own = seq[n][0]
                p = n % 2
                t = TSet(own, h)
                smh, rsm = sm[p][h], r_sm[p][h]
                for c in range(4):
                    cs = slice(128 * c, 128 * (c + 1))
                    S.op("dve", lambda e, cs=cs: e.tensor_tensor_scan(out=t.bb[:, cs], data0=ones512[:, cs],
                                                                     data1=t.gg[:, cs], initial=0.0,
                                                                     op0=ALU.mult, op1=ALU.add),
                         [t.r_gg, r_o512], [t.r_bb])
                yield
                if own:
                    act(t.eq[:], t.bb[:], AF.Exp, [t.r_bb], [t.r_eq])
                act(t.ek[:], t.bb[:], AF.Exp, [t.r_bb], [t.r_ek], scale=-1.0)
                if not own:
                    act(smh[:, 12:16], t.bb[:, 127:512:128], AF.Exp, [t.r_bb], [rsm])
                yield
                if own:
                    cp(smh[:, 12:16], t.eq[:, 127:512:128], [t.r_eq], [rsm])
                    tt(Qt[p][h][:], t.qq[:], t.eq[:], ALU.mult, [t.r_qq, t.r_eq], [r_Qt[p][h]])
                    stt(Kt[p][h][:], t.kr[:], lbt[:, 4 + h:5 + h], t.ek[:], ALU.mult, ALU.mult,
                        [t.r_kr, r_lb, t.r_ek], [r_Kt[p][h]])
                else:
                    ts(smh[:, 8:12], smh[:, 12:16], lbt[:, 4 + h:5 + h], None, ALU.mult, ALU.bypass, [rsm, r_lb], [rsm])
                    for c in range(4):
                        cs = slice(128 * c, 128 * (c + 1))
                        stt(Kt[p][h][:, cs], t.kr[:, cs], smh[:, 8 + c:9 + c], t.ek[:, cs], ALU.mult, ALU.mult,
                            [t.r_kr, rsm, t.r_ek], [r_Kt[p][h]])
                yield

            def stage_ab(n):
                own = seq[n][0]
                p = n % 2
                xsrc, r_x = xs[n % NXS], r_xs[n % NXS]
                heads = {}

                def step():
                    for hh in list(heads):
                        try:
                            next(heads[hh])
                        except StopIteration:
                            del heads[hh]

                nset = 2 if own else 3
                for h in range(H):
                    t = TSet(own, h)
                    while (h - nset) in heads:
                        step()
                        yield
                    bank, rb = proj_fm(xsrc, r_x, CF + 128 * h)
                    act(t.gg[:], bank[:], AF.Exp, [rb], [t.r_gg], scale=-1.0)
                    act(t.gg[:], t.gg[:], AF.Ln, [t.r_gg], [t.r_gg], scale=1.0, bias=1.0)
                    act(t.gg[:], t.gg[:], AF.Exp, [t.r_gg], [t.r_gg], scale=-1.0)
                    act(t.kr[:], t.gg[:], AF.Identity, [t.r_gg], [t.r_kr], scale=-1.0, bias=1.0)
                    act(t.gg[:], t.gg[:], AF.Ln, [t.r_gg, r_lb], [t.r_gg],
                        scale=lbt[:, 4 + h:5 + h], bias=lbt[:, h:h + 1])
                    if own:
                        step()
                        yield
                    if own:
                        bank, rb = proj_fm(xsrc, r_x, CQ + 128 * h)
                        act(t.qq[:], bank[:], AF.Copy, [rb], [t.r_qq])
                        step()
                        yield
                    heads[h] = gate_head(n, h)
                    g = h
                    i = pj_alloc()
                    for kc in range(KC):
                        mm(PJ[i][:], xsrc[:, kc, 128 * g:128 * (g + 1)], win[2][:, kc, :],
                           kc == 0, kc == KC - 1, [r_win[2], r_x], [r_PJ[i]], kc == KC - 1)
                    act(vtok[p][:, g, :], PJ[i][:], AF.Copy, [r_PJ[i]], [r_vtok[p][g]])
                    step()
                    yield
                while heads:
                    step()
                    yield

            def stage_c_prev(n):
                p = n % 2
                HS = [slice(128 * h, 128 * (h + 1)) for h in range(H)]
                ktk = [Ktok[0], Ktok[1], scm[0], scm[1]]
                r_ktk = [r_Ktok[0], r_Ktok[1], r_scm[0], r_scm[1]]
                stb = [ST, OO, SC, ST]
                r_stb = [r_ST, r_OO, r_SC, r_ST]

                def tr2(g0):
                    for g in (g0, g0 + 1):
                        gs = slice(128 * g, 128 * (g + 1))
                        for h in range(H):
                            o0 = 512 * (g - g0) + 128 * h
                            S.op("pe", lambda e, h=h, gs=gs, o0=o0: e.transpose(TR[:, o0:o0 + 128], Kt[p][h][:, gs], ident[:]),
                                 [r_Kt[p][h], r_ident], [r_TR], signal=(h == H - 1 and g == g0 + 1))

                def cp2(g0):
                    for g in (g0, g0 + 1):
                        cp(ktk[g][:], TR[:, 512 * (g - g0):512 * (g - g0 + 1)], [r_TR], [r_ktk[g]])

                def st(g):
                    for h in range(H):
                        mm(stb[g][:, HS[h]], ktk[g][:, HS[h]], vtok[p][:, g, HS[h]], True, True,
                           [r_ktk[g], r_vtok[p][g]], [r_stb[g]], h == H - 1)

                def upd(g):
                    for h in range(H):
                        stt(Sf[:, HS[h]], Sf[:, HS[h]], sm[p][h][:, 12 + g:13 + g], stb[g][:, HS[h]], ALU.mult, ALU.add,
                            [r_stb[g], r_Sf, r_sm[p][h]], [r_Sf])

                tr2(0)
                yield
                cp2(0)
                yield
                tr2(2)
                st(0)
                st(1)
                yield
                cp2(2)
                upd(0)
                yield
                st(2)
                st(3)
                upd(1)
                yield
                upd(2)
                yield
                upd(3)
                act(Sb[:], Sf[:], AF.Copy, [r_Sf], [r_Sb])
                yield

            def stage_c_pipe(n, SQ, r_SQ):
                own, blk = seq[n]
                p = n % 2

                def grp(g):
                    gs = slice(128 * g, 128 * (g + 1))
                    kb = g % 2
                    HS = [slice(128 * h, 128 * (h + 1)) for h in range(H)]
                    for h in range(H):
                        S.op("pe", lambda e, h=h: e.transpose(TR[:, HS[h]], Kt[p][h][:, gs], ident[:]),
                             [r_Kt[p][h], r_ident], [r_TR], signal=(h == H - 1))
                    if own:
                        for h in range(H):
                            mm(SC[:, HS[h]], Kt[p][h][:, gs], Qt[p][h][:, gs], True, True,
                               [r_Kt[p][h], r_Qt[p][h]], [r_SC], h == H - 1)
                    yield
                    cp(Ktok[kb][:], TR[:, 0:512], [r_TR], [r_Ktok[kb]])
                    if own:
                        tt(scm[kb][:], SC[:], mask4, ALU.mult, [r_SC, r_cst], [r_scm[kb]])
                    yield
                    if own:
                        for h in range(H):
                            mm(OO[:, HS[h]], vtok[p][:, g, HS[h]], scm[kb][:, HS[h]], True, False,
                               [r_vtok[p][g], r_scm[kb]], [r_OO], False)
                            mm(OO[:, HS[h]], Sb[:, HS[h]], Qt[p][h][:, gs], False, True, [r_Sb, r_Qt[p][h]], [r_OO], h == H - 1)
                    for h in range(H):
                        mm(ST[:, HS[h]], Ktok[kb][:, HS[h]], vtok[p][:, g, HS[h]], True, True,
                           [r_Ktok[kb], r_vtok[p][g]], [r_ST], h == H - 1)
                    yield
                    if own:
                        tt(Uf[:], ST[:], Sf[:], ALU.add, [r_ST, r_Sf], [r_Uf])
                        act(og_[:], OO[:], AF.Copy, [r_OO], [r_og])
                        act(o2_[:], OO[:], AF.Square, [r_OO], [r_o2])
                    else:
                        for h in range(H):
                            stt(Sf[:, HS[h]], Sf[:, HS[h]], sm[p][h][:, 12 + g:13 + g], ST[:, HS[h]], ALU.mult, ALU.add,
                                [r_ST, r_Sf, r_sm[p][h]], [r_Sf])
                        if g == 3:
                            act(Sb[:], Sf[:], AF.Copy, [r_Sf], [r_Sb])
                        return
                    yield
                    for h in range(H):
                        dcol = sm[p][h][:, 12 + g:13 + g]
                        ts(Sf[:, HS[h]], Uf[:, HS[h]], dcol, None, ALU.mult, ALU.bypass, [r_Uf, r_sm[p][h]], [r_Sf])
                        act(Sb[:, HS[h]], Uf[:, HS[h]], AF.Copy, [r_Uf, r_sm[p][h]], [r_Sb], scale=dcol)
                    mm(SQ[:], ones_b[:], o2_[:], True, True, [r_ones, r_o2], [r_SQ], True)
                    yield
                    act(rs_[:], SQ[:], AF.Ln, [r_SQ], [r_rs], scale=1.0 / 128.0, bias=EPS)
                    act(rs_[:], rs_[:], AF.Exp, [r_rs], [r_rs], scale=-0.5)
                    yield
                    tt(og_[:], og_[:], rs_[:], ALU.mult, [r_og, r_rs], [r_og])
                    for h in range(H):
                        stt(mixT[:, h, blk * TB + 128 * g: blk * TB + 128 * (g + 1)], sil2[p][h][:, gs],
                            prm[:, PGN + h:PGN + h + 1], og_[:, HS[h]], ALU.mult, ALU.mult,
                            [r_sil2[p][h], r_prm, r_og], [r_mix[h][blk]])
                    yield

                lag = 3 if own else 2
                gens = {}
                rnd = -2
                started = 0
                while started < 4 or gens:
                    if started < 4 and rnd >= started * lag - 2:
                        gens[started] = grp(started)
                        started += 1
                    for g in sorted(gens):
                        try:
                            next(gens[g])
                        except StopIteration:
                            del gens[g]
                    rnd += 1
                    yield

            def stage_c(n):
                own, blk = seq[n]
                p = n % 2

                def pre(g):
                    gs = slice(128 * g, 128 * (g + 1))
                    kb = g % 2
                    for h in range(H):
                        hs = slice(128 * h, 128 * (h + 1))
                        S.op("pe", lambda e, h=h, hs=hs: e.transpose(TR[:, hs], Kt[p][h][:, gs], ident[:]),
                             [r_Kt[p][h], r_ident], [r_TR], signal=(h == H - 1))
                    if own:
                        for h in range(H):
                            hs = slice(128 * h, 128 * (h + 1))
                            mm(SC[:, hs], Kt[p][h][:, gs], Qt[p][h][:, gs], True, True,
                               [r_Kt[p][h], r_Qt[p][h]], [r_SC], h == H - 1)
                    yield
                    cp(Ktok[kb][:], TR[:, 0:512], [r_TR], [r_Ktok[kb]])
                    if own:
                        tt(scm[kb][:], SC[:], mask4, ALU.mult, [r_SC, r_cst], [r_scm[kb]])
                    yield

                def main(g):
                    gs = slice(128 * g, 128 * (g + 1))
                    kb = g % 2
                    if own:
                        for h in range(H):
                            hs = slice(128 * h, 128 * (h + 1))
                            mm(OO[:, hs], vtok[p][:, g, hs], scm[kb][:, hs], True, False,
                               [r_vtok[p][g], r_scm[kb]], [r_OO], False)
                            mm(OO[:, hs], Sb[:, hs], Qt[p][h][:, gs], False, True, [r_Sb, r_Qt[p][h]], [r_OO], h == H - 1)
                    for h in range(H):
                        hs = slice(128 * h, 128 * (h + 1))
                        mm(ST[:, hs], Ktok[kb][:, hs], vtok[p][:, g, hs], True, True,
                           [r_Ktok[kb], r_vtok[p][g]], [r_ST], h == H - 1)
                    yield
                    if own:
                        tt(Uf[:], ST[:], Sf[:], ALU.add, [r_ST, r_Sf], [r_Uf])
                        act(og_[:], OO[:], AF.Copy, [r_OO], [r_og])
                        act(o2_[:], OO[:], AF.Square, [r_OO], [r_o2])
                    else:
                        for h in range(H):
                            hs = slice(128 * h, 128 * (h + 1))
                            stt(Sf[:, hs], Sf[:, hs], sm[p][h][:, 12 + g:13 + g], ST[:, hs], ALU.mult, ALU.add,
                                [r_ST, r_Sf, r_sm[p][h]], [r_Sf])
                    yield
                    if own:
                        for h in range(H):
                            hs = slice(128 * h, 128 * (h + 1))
                            dcol = sm[p][h][:, 12 + g:13 + g]
                            ts(Sf[:, hs], Uf[:, hs], dcol, None, ALU.mult, ALU.bypass, [r_Uf, r_sm[p][h]], [r_Sf])
                            act(Sb[:, hs], Uf[:, hs], AF.Copy, [r_Uf, r_sm[p][h]], [r_Sb], scale=dcol)
                    else:
                        act(Sb[:], Sf[:], AF.Copy, [r_Sf], [r_Sb])
                    if own:
                        mm(SQ[:], ones_b[:], o2_[:], True, True, [r_ones, r_o2], [r_SQ], True)
                    yield
                    if own:
                        act(rs_[:], SQ[:], AF.Ln, [r_SQ], [r_rs], scale=1.0 / 128.0, bias=EPS)
                        act(rs_[:], rs_[:], AF.Exp, [r_rs], [r_rs], scale=-0.5)
                        yield
                        tt(og_[:], og_[:], rs_[:], ALU.mult, [r_og, r_rs], [r_og])
                        for h in range(H):
                            hs = slice(128 * h, 128 * (h + 1))
                            stt(mixT[:, h, blk * TB + 128 * g: blk * TB + 128 * (g + 1)], sil2[p][h][:, gs],
                                prm[:, PGN + h:PGN + h + 1], og_[:, hs], ALU.mult, ALU.mult,
                                [r_sil2[p][h], r_prm, r_og], [r_mix[h][blk]])
                        yield

                yield from pre(0)
                for g in range(4):
                    gm = main(g)
                    gp = pre(g + 1) if g + 1 < 4 else None
                    alive = True
                    while alive:
                        alive = False
                        try:
                            next(gm)
                            alive = True
                        except StopIteration:
                            pass
                        if gp is not None:
                            try:
                                next(gp)
                                alive = True
                            except StopIteration:
                                gp = None
                        if alive:
                            yield

            def stage_og(n):
                xsrc, r_x = xs[n % NXS], r_xs[n % NXS]
                for h in range(H):
                    bank, rb = proj_fm(xsrc, r_x, COG + 128 * h)
                    sl, rsl = sil2[n % 2][h], r_sil2[n % 2][h]
                    act(sl[:], bank[:], AF.Exp, [rb], [rsl], scale=-1.0)
                    act(sl[:], sl[:], AF.Ln, [rsl], [rsl], scale=1.0, bias=1.0)
                    act(sl[:], sl[:], AF.Exp, [rsl], [rsl], scale=-1.0)
                    yield
                    tt(sl[:], bank[:], sl[:], ALU.mult, [rb, rsl], [rsl])
                    yield

            def stage_d(n, cts=(0, 1, 2, 3), delay=0):
                own, blk = seq[n]
                xsrc, r_x = xs[n % NXS], r_xs[n % NXS]
                for _ in range(delay):
                    yield
                for ct in cts:
                    zi = ct % 2
                    cw = lambda j: prm[:, PCW + 4 * j + ct:PCW + 4 * j + ct + 1]
                    if blk == 0:
                        bank, rb = proj_fm(xh, r_xh, CU + 128 * ct, ncol=2)
                        cp(uh[:], bank[:, 0:2], [rb], [r_uh])
                        yield
                        bank, rb = proj_fm(xh, r_xh, CC + 128 * ct, ncol=2)
                        tt(zc[:, ct, :], bank[:, 0:2], uh[:], ALU.mult, [rb, r_uh], [r_zc[ct]])
                        yield
                    bank, rb = proj_fm(xsrc, r_x, CU + 128 * ct)
                    act(u_sb[:], bank[:], AF.Copy, [rb], [r_u])
                    yield
                    bank, rb = proj_fm(xsrc, r_x, CC + 128 * ct)
                    yield
                    tt(z_ext[zi][:, 2:TB + 2], bank[:], u_sb[:], ALU.mult, [rb, r_u], [r_z[zi]])
                    cp(z_ext[zi][:, 0:2], zc[:, ct, :], [r_zc[ct]], [r_z[zi]])
                    cp(zc[:, ct, :], z_ext[zi][:, TB:TB + 2], [r_z[zi]], [r_zc[ct]])
                    yield
                    ts(acc[zi][:], z_ext[zi][:, 2:TB + 2], cw(2), None, ALU.mult, ALU.bypass,
                       [r_z[zi], r_prm], [r_acc[zi]])
                    yield
                    stt(acc[zi][:], z_ext[zi][:, 1:TB + 1], cw(1), acc[zi][:], ALU.mult, ALU.add,
                        [r_z[zi], r_prm, r_acc[zi]], [r_acc[zi]])
                    stt(acc[zi][:], z_ext[zi][:, 0:TB], cw(0), acc[zi][:], ALU.mult, ALU.add,
                        [r_z[zi], r_prm, r_acc[zi]], [r_acc[zi]])
                    yield
                    bank, rb = proj_fm(xsrc, r_x, CB + 128 * ct)
                    yield
                    tt(mixT[:, 4 + ct, blk * TB:(blk + 1) * TB], bank[:], acc[zi][:], ALU.mult, [rb, r_acc[zi]],
                       [r_mix[4 + ct][blk]])
                    yield

            issue_x(0)
            for h in range(H):
                S.load("pool", win[1][:, :, 128 * h:128 * (h + 1)], k3(win_d[1])[:, :, 128 * h:128 * (h + 1)], r_win1h[h])
            S.load("pool", win[2][:], k3(win_d[2]), r_win[2])
            issue_x(1)
            S.load("pool", xh[:], xT_v[:, :, 0:2], r_xh)
            for i in (0, 3, 4, 5, 6):
                S.load("pool", win[i][:], k3(win_d[i]), r_win[i])

            def drive(gens, periods=None):
                periods = periods or [1] * len(gens)
                live = [[g, pd] for g, pd in zip(gens, periods) if g is not None]
                rnd = 0
                while live:
                    for it in list(live):
                        if rnd % it[1] == 0 or len(live) == 1:
                            try:
                                next(it[0])
                            except StopIteration:
                                live.remove(it)
                    rnd += 1

            def wff1_load(i):
                S.load("pool", reg[i][:], k3(wff1_d[i]), r_wff1[i])

            def wff2_load(i):
                S.load("pool", wff2c[i][:], wff2_d[i].rearrange("p (f c) -> p f c", f=4), r_wff2[i])

            drive([stage_ab(0)])
            for n in range(NBLK):
                if n + 2 < NBLK:
                    issue_x(n + 2)
                own = seq[n][0]
                g_ab = stage_ab(n + 1) if n + 1 < NBLK else None

                if not own:
                    g_c = stage_c_prev(n)
                elif n == NBLK - 1:
                    pj_n[0] = NPJ - 1
                    g_c = stage_c_pipe(n, PJ[NPJ - 1], r_PJ[NPJ - 1])
                else:
                    g_c = stage_c(n)
                c_done = [False]

                def c_wrap():
                    yield from g_c
                    c_done[0] = True

                def ab_then_og(n=n):
                    if g_ab is not None:
                        yield from g_ab
                    if n + 1 < NBLK and seq[n + 1][0]:
                        yield from stage_og(n + 1)

                if own:
                    drive([ab_then_og(), c_wrap(), stage_d(n, (0, 2)), stage_d(n, (1, 3), 3)], [1, 1, 1, 1])
                else:
                    drive([ab_then_og(), c_wrap()], [1, 1])
                if n == NBLK - 3:
                    r_wff2[0].r = {k: S.cnt[k] for k in ("pe", "act", "dve")}
                    wff2_load(0)
                if n == NBLK - 2:
                    fence6 = {k: S.cnt[k] for k in ("pe", "act", "dve")}
                    for kind, i in (("1", 0), ("1", 1), ("2", 1), ("1", 2), ("2", 2), ("1", 3), ("2", 3)):
                        if kind == "1":
                            r_wff1[i].r = dict(fence6)
                            wff1_load(i)
                        else:
                            r_wff2[i].r = dict(fence6)
                            wff2_load(i)
            fence7 = {k: S.cnt[k] for k in ("pe", "act", "dve")}
            lazy = []
            lazy_left = {}
            for kind, i in (("1", 4), ("2", 4), ("1", 5), ("2", 5), ("1", 6), ("2", 6), ("1", 7), ("2", 7)):
                (r_wff1 if kind == "1" else r_wff2)[i].r = dict(fence7)
                lazy_left[(kind, i)] = 1
                lazy.append((kind, i, 0))

            def pump(k=1):
                for _ in range(k):
                    if not lazy:
                        return
                    kind, i, q = lazy.pop(0)
                    if kind == "1":
                        wff1_load(i)
                    else:
                        wff2_load(i)
                    lazy_left[(kind, i)] -= 1

            def ensure(kind, i):
                while lazy_left.get((kind, i), 0) > 0:
                    pump(1)

        with ExitStack() as p2:
            fence = {k: S.cnt[k] for k in ("pe", "act", "dve") if S.cnt[k] > 0}

            def Res2(name):
                r = Res(name)
                r.r = dict(fence)
                return r

            NWO = KC
            wo = [sb("wo%d" % i, [128, KC, 128], BF16, p2) for i in range(NWO)]
            r_wo = [Res2("wo%d" % i) for i in range(NWO)]
            hA = [sb("hA%d" % i, [128, KC, T2], F32, p2) for i in range(2)]
            r_hA = [[Res2("hA%d_%d" % (i, j)) for j in range(KC)] for i in range(2)]
            h1b = [sb("h1b%d" % i, [128, KC, T2], BF16, p2) for i in range(2)]
            r_h1b = [[Res2("h1b%d_%d" % (i, j)) for j in range(KC)] for i in range(2)]
            NSQ = 2
            hsq = [sb("hsq%d" % i, [128, T2], BF16, p2) for i in range(NSQ)]
            r_hsq = [Res2("hsq%d" % i) for i in range(NSQ)]
            hbc = [sb("hbc%d" % i, [128, T2], BF16, p2) for i in range(NSQ)]
            r_hbc = [Res2("hbc%d" % i) for i in range(NSQ)]
            mean = sb("mean", [128, T2], F32, p2)
            rstd = sb("rstd", [128, T2], F32, p2)
            nmr = sb("nmr", [128, T2], F32, p2)
            r_mean, r_rstd, r_nmr = Res2("mean"), Res2("rstd"), Res2("nmr")
            NHID = 4
            hid = [sb("hid%d" % i, [128, T2], BF16, p2) for i in range(NHID)]
            r_hid = [Res2("hid%d" % i) for i in range(NHID)]
            OP = ps("OP", [128, 512], F32, p2)
            r_OP = Res2("OP")
            SS = ps("SS", [128, 512], F32, p2)
            r_SS = Res2("SS")
            F1 = [ps("F1%d" % i, [128, 512], F32, p2) for i in range(2)]
            r_F1 = [Res2("F10"), Res2("F11")]
            AC = [ps("AC%d" % i, [128, 512], F32, p2) for i in range(4)]
            r_AC = [Res2("AC%d" % j) for j in range(4)]

            wo_n = [0]
            wo_loaded = [False] * KC

            OPb = [OP, SS]
            r_OPb = [r_OP, r_SS]

            def ln_stats(buf, rbuf):
                for k in range(KC + 2):
                    if k >= 2:
                        j = k - 2
                        q = j % NSQ
                        mm(SS[:, 0:T2], ones_b[:], hbc[q][:], j == 0, j == KC - 1,
                           [r_ones, r_hbc[q]], [r_SS], True, skip_group_check=True)
                        mm(SS[:, T2:2 * T2], ones_b[:], hsq[q][:], False, j == KC - 1,
                           [r_ones, r_hsq[q]], [r_SS], True, skip_group_check=True)
                    if k < KC:
                        q = k % NSQ
                        act(hbc[q][:], hA[buf][:, k, :], AF.Copy, [rbuf[k]], [r_hbc[q]])
                        act(hsq[q][:], hA[buf][:, k, :], AF.Square, [rbuf[k]], [r_hsq[q]])
                    yield
                ts(mean[:], SS[:, 0:T2], 1.0 / D, None, ALU.mult, ALU.bypass, [r_SS], [r_mean])
                tt(nmr[:], mean[:], mean[:], ALU.mult, [r_mean], [r_nmr])
                stt(rstd[:], SS[:, T2:2 * T2], 1.0 / D, nmr[:], ALU.mult, ALU.subtract, [r_SS, r_nmr], [r_rstd])
                yield
                act(rstd[:], rstd[:], AF.Ln, [r_rstd], [r_rstd], scale=1.0, bias=EPS)
                act(rstd[:], rstd[:], AF.Exp, [r_rstd], [r_rstd], scale=-0.5)
                yield
                stt(nmr[:], mean[:], -1.0, rstd[:], ALU.mult, ALU.mult, [r_mean, r_rstd], [r_nmr])
                yield

            N_STATS = KC + 5

            def normalize(buf, rh, pg, pb, post, on_act=False):
                for k in range(KC + 1):
                    if k < KC:
                        tt(hA[buf][:, k, :], hA[buf][:, k, :], rstd[:], ALU.mult, [rh[k], r_rstd], [rh[k]])
                    yield
                    if k < KC:
                        tt(hA[buf][:, k, :], hA[buf][:, k, :], nmr[:], ALU.add, [rh[k], r_nmr], [rh[k]])
                        if on_act:
                            act(hA[buf][:, k, :], hA[buf][:, k, :], AF.Identity, [rh[k], r_prm], [rh[k]],
                                scale=prm[:, pg + k:pg + k + 1], bias=prm[:, pb + k:pb + k + 1])
                        else:
                            ts(hA[buf][:, k, :], hA[buf][:, k, :], prm[:, pg + k:pg + k + 1], prm[:, pb + k:pb + k + 1],
                               ALU.mult, ALU.add, [rh[k], r_prm], [rh[k]])
                    if k >= 1:
                        post(k - 1)
                    yield

            N_NORM = 2 * (KC + 1)

            def prologue(b):
                buf = b % 2
                tok = slice(b * T2, (b + 1) * T2)
                rh = r_hA[buf]
                wsl = {}

                def wo_load(j):
                    wsl[j] = j
                    if not wo_loaded[j]:
                        wo_loaded[j] = True
                        S.load("pool", wo[j][:], wout_d[j].rearrange("p (k c) -> p k c", k=KC), r_wo[j])

                for j in range(NWO):
                    wo_load(j)
                yield
                for j in range(KC):
                    S.load("sp", hA[buf][:, j, :], xT_v[:, j, 2 + b * T2:2 + (b + 1) * T2], rh[j])
                yield
                for k in range(KC + 1):
                    if k >= 1 and k + NWO - 1 < KC:
                        wo_load(k + NWO - 1)
                    if k < KC:
                        w = wsl[k]
                        for c in range(KC):
                            mm(OPb[k % 2][:, 0:T2], wo[w][:, c, :], mixT[:, c, tok], c == 0, c == KC - 1,
                               [r_wo[w], r_mix[c][b // 2]], [r_OPb[k % 2]], c == KC - 1)
                    if k >= 1:
                        j = k - 1
                        stt(hA[buf][:, j, :], hA[buf][:, j, :], ALPHA, OPb[j % 2][:, 0:T2], ALU.mult, ALU.add,
                            [r_OPb[j % 2], rh[j]], [rh[j]])
                    yield
                yield from ln_stats(buf, rh)

                def post(j):
                    act(h1b[buf][:, j, :], hA[buf][:, j, :], AF.Copy, [rh[j]], [r_h1b[buf][j]])

                yield from normalize(buf, rh, PG1, PB1, post, on_act=(b == 0))

            N_PRO = 2 + (KC + 1) + N_STATS + N_NORM

            def ffn(b):
                buf = b % 2

                def ffn1(f):
                    half = f % 2
                    ensure("1", f // 4)
                    for c in range(KC):
                        mm(F1[half][:, 0:T2], reg[f // 4][:, c, 128 * (f % 4):128 * (f % 4 + 1)], h1b[buf][:, c, :],
                           c == 0, c == KC - 1, [r_wff1[f // 4], r_h1b[buf][c]], [r_F1[half]], c == KC - 1)
                    act(F1[half][:, 0:T2], F1[half][:, 0:T2], AF.Relu, [r_F1[half]], [r_F1[half]])
                    act(hid[f % NHID][:], F1[half][:, 0:T2], AF.Square, [r_F1[half]], [r_hid[f % NHID]])

                def ffn2(f):
                    ensure("2", f // 4)
                    for j in range(KC):
                        half = j % 2
                        mm(AC[j // 2][:, half * T2:(half + 1) * T2], wff2c[f // 4][:, f % 4, 128 * j:128 * (j + 1)], hid[f % NHID][:],
                           (f == 0 and half == 0), f == NF - 1, [r_wff2[f // 4], r_hid[f % NHID]], [r_AC[j // 2]],
                           (f == NF - 1) or (j == KC - 1), skip_group_check=True)

                ffn1(0)
                ffn1(1)
                for f in range(NF):
                    ffn2(f)
                    if f + 2 < NF:
                        ffn1(f + 2)
                    yield

            def resid2(b):
                buf = b % 2
                rh = r_hA[buf]
                for j in range(KC):
                    half = j % 2
                    stt(hA[buf][:, j, :], hA[buf][:, j, :], ALPHA, AC[j // 2][:, half * T2:(half + 1) * T2],
                        ALU.mult, ALU.add, [r_AC[j // 2], rh[j]], [rh[j]])

            def epilogue(b):
                buf = b % 2
                tok = slice(b * T2, (b + 1) * T2)
                rh = r_hA[buf]
                yield from ln_stats(buf, rh)

                def post(j):
                    S.store("sp", outT_v[:, j, tok], hA[buf][:, j, :], rh[j])

                yield from normalize(buf, rh, PG2, PB2, post, on_act=(b == NB2 - 1))

            N_EPI = N_STATS + N_NORM

            def run(gen, n=None):
                if gen is None:
                    return False
                k = 0
                while n is None or k < n:
                    try:
                        next(gen)
                    except StopIteration:
                        return False
                    k += 1
                return True

            g_p0 = prologue(0)
            run(g_p0, 1)
            pump(len(lazy))
            run(g_p0)
            for b in range(NB2):
                g_f = ffn(b)
                chain = []
                n_steps = 0
                if b > 0:
                    chain.append(epilogue(b - 1))
                    n_steps += N_EPI
                if b + 1 < NB2:
                    chain.append(prologue(b + 1))
                    n_steps += N_PRO
                done = 0
                if b + 1 < NB2:
                    g_pro = chain[-1]
                    run(g_pro, 1)
                    done += 1
                for f in range(NF):
                    run(g_f, 1)
                    pump(1)
                    target = min(n_steps, -(-(f + 1) * n_steps // (NF - 2)))
                    while done < target and chain:
                        if run(chain[0], 1):
                            done += 1
                        else:
                            chain.pop(0)
                run(g_f)
                for g_ in chain:
                    run(g_)
                resid2(b)
            run(epilogue(NB2 - 1))
            S.finish("sp")
    return nc


def _layout_inputs(x, w_in, lb_logits, gate_norm_w, conv_w, w_out, ln1_g, ln1_b, w_ff1, w_ff2, ln2_g, ln2_b):
    f = lambda a: np.ascontiguousarray(np.asarray(a, dtype=np.float32))
    x = f(x)
    fm4 = lambda v: f(v).reshape(4, 128).T
    fm8 = lambda v: f(v).reshape(8, 128).T
    prm = np.zeros((128, NPRM), np.float32)
    prm[:, PL0:PL0 + 4] = fm4(lb_logits[0])
    prm[:, PL1:PL1 + 4] = fm4(lb_logits[1])
    prm[:, PGN:PGN + 4] = fm4(gate_norm_w[0])
    for j in range(3):
        prm[:, PCW + 4 * j:PCW + 4 * j + 4] = fm4(conv_w[0, j])
    prm[:, PG1:PG1 + 8] = fm8(ln1_g[0])
    prm[:, PB1:PB1 + 8] = fm8(ln1_b[0])
    prm[:, PG2:PG2 + 8] = fm8(ln2_g[0])
    prm[:, PB2:PB2 + 8] = fm8(ln2_b[0])
    cst = np.zeros((128, 640), np.float32)
    cst[:, 0:128] = np.eye(128, dtype=np.float32)
    tri = np.triu(np.ones((128, 128), np.float32))
    cst[:, 128:640] = np.tile(tri, (1, 4))
    wo_l = f(f(w_out[0]).reshape(KC, 128, KC, 128).transpose(2, 1, 0, 3).reshape(KC, 128, KC * 128))
    chunked = lambda w, n: f(f(w).reshape(KC, 128, n, 512).transpose(2, 1, 0, 3).reshape(n, 128, KC * 512))
    w2_l = f(f(w_ff2[0]).reshape(8, 4, 128, D).transpose(0, 2, 1, 3).reshape(8, 128, 4 * D))
    shared = {"w_in": chunked(w_in[0], 7), "w_out": wo_l, "w_ff1": chunked(w_ff1[0], 8), "w_ff2": w2_l,
              "prm": prm, "cst": cst}
    in_maps = []
    for c in range(N_CORES):
        b, half = c // 2, c % 2
        xT = np.zeros((D, T + 2), np.float32)
        xpT = np.zeros((D, T), np.float32)
        xT[:, 2:] = x[b, half * T:(half + 1) * T, :].T
        if half == 1:
            xT[:, 0:2] = x[b, T - 2:T, :].T
            xpT[:, :] = x[b, 0:T, :].T
        blocked = lambda a: f(a.reshape(KC, 128, NB1, TB).transpose(2, 1, 0, 3).reshape(NB1, 128, KC * TB))
        m = dict(shared)
        m["xT"] = xT
        m["xb"] = blocked(xT[:, 2:])
        m["xpb"] = blocked(xpT)
        in_maps.append(m)
    return in_maps


_NC_CACHE = {}


def kernel(x, w_in, lb_logits, gate_norm_w, conv_w, w_out, ln1_g, ln1_b, w_ff1, w_ff2, ln2_g, ln2_b):
    in_maps = _layout_inputs(x, w_in, lb_logits, gate_norm_w, conv_w, w_out, ln1_g, ln1_b,
                             w_ff1, w_ff2, ln2_g, ln2_b)
    if "nc" not in _NC_CACHE:
        _NC_CACHE["nc"] = build_program()
    res = run_bass_kernel_spmd(_NC_CACHE["nc"], in_maps, core_ids=list(range(N_CORES)))
    out = np.empty((4, 2 * T, D), np.float32)
    for c in range(N_CORES):
        b, half = c // 2, c % 2
        out[b, half * T:(half + 1) * T, :] = np.asarray(res.results[c]["outT"]).T
    return out
```

```python
import numpy as np
from contextlib import ExitStack
import concourse.bass as bass
import concourse.mybir as mybir
from concourse.bass_utils import run_bass_kernel_spmd

F32 = mybir.dt.float32
F32R = mybir.dt.float32r
BF16 = mybir.dt.bfloat16
AF = mybir.ActivationFunctionType
ALU = mybir.AluOpType

N_CORES = 8
D = 1024
KC = 8
T = 2048
TB = 512
NB1 = T // TB
T2 = 256
NB2 = T // T2
H = 4
DFF = 4096
NF = DFF // 128
ALPHA = 2.0 ** 0.25
EPS = 1e-5
CQ, CF, CI, COG, CB, CC, CU = 0, 512, 1024, 1536, 2048, 2560, 3072
PL0, PL1, PGN, PCW, PG1, PB1, PG2, PB2, NPRM = 0, 4, 8, 12, 24, 32, 40, 48, 56

SAME_ENGINE_SYNC = True


class Res:
    __slots__ = ("name", "w", "r", "dsem", "dcnt", "wtick", "rtick")

    def __init__(self, name):
        self.name = name
        self.w = None
        self.r = {}
        self.dsem = None
        self.dcnt = 0
        self.wtick = -1
        self.rtick = -1


class Sched:
    def __init__(self, nc, es):
        self.nc = nc
        self.es = es
        self.engs = {"pe": nc.tensor, "act": nc.scalar, "dve": nc.vector, "pool": nc.gpsimd, "sp": nc.sync}
        self.sem = {k: es.enter_context(nc.semaphore("s_" + k)) for k in self.engs}
        self.cnt = {k: 0 for k in self.engs}
        self.seen = {k: {} for k in self.engs}
        self.semobj = {}
        self.nsem = 0
        self.store_res = []
        self.tick = 0

    def _semof(self, key):
        if isinstance(key, str) and key in self.sem:
            return self.sem[key]
        return self.semobj[key]

    def _wait(self, eng, deps):
        best = {}
        for key, val in deps:
            if key == eng and (eng == "pe" or not SAME_ENGINE_SYNC):
                continue
            if val > best.get(key, 0):
                best[key] = val
        for key, val in best.items():
            if key in self.cnt:
                assert val <= self.cnt[key], "wait on not-yet-signalled %s seq %d (cnt %d)" % (key, val, self.cnt[key])
            if val > self.seen[eng].get(key, 0):
                self.engs[eng].wait_ge(self._semof(key), val)
                self.seen[eng][key] = val

    def _deps(self, reads, writes):
        deps = []
        for r in reads:
            if r.w is not None:
                deps.append(r.w)
        for w in writes:
            if w.w is not None:
                deps.append(w.w)
            deps.extend(w.r.items())
        return deps

    def op(self, eng, fn, reads=(), writes=(), signal=True):
        self._wait(eng, self._deps(reads, writes))
        ins = fn(self.engs[eng])
        self.tick += 1
        seq = self.cnt[eng] + 1
        if signal:
            ins.then_inc(self.sem[eng], 1)
            self.cnt[eng] = seq
        for r in reads:
            if r not in writes:
                r.r[eng] = max(r.r.get(eng, 0), seq)
                r.rtick = self.tick
        for w in writes:
            w.w = (eng, seq)
            w.r = {}
            w.wtick = self.tick
            w.rtick = -1
        return ins

    def _dsem(self, res):
        if res.dsem is None:
            self.nsem += 1
            key = "d%d_%s" % (self.nsem, res.name)
            self.semobj[key] = self.es.enter_context(self.nc.semaphore(key))
            res.dsem = key
        return res.dsem

    def load(self, q, out, in_, res):
        self._wait(q, self._deps((), (res,)))
        key = self._dsem(res)
        res.dcnt += 16
        self.engs[q].dma_start(out=out, in_=in_).then_inc(self.semobj[key], 16)
        res.w = (key, res.dcnt)
        res.r = {}

    def store(self, q, out, in_, res):
        self._wait(q, self._deps((res,), ()))
        key = self._dsem(res)
        res.dcnt += 16
        self.engs[q].dma_start(out=out, in_=in_).then_inc(self.semobj[key], 16)
        res.r[key] = res.dcnt
        if res not in self.store_res:
            self.store_res.append(res)

    def finish(self, q="sp"):
        for res in self.store_res:
            self.engs[q].wait_ge(self.semobj[res.dsem], res.dcnt)


def build_program():
    nc = bass.Bass("TRN2", target_bir_lowering=False)
    xT_d = nc.dram_tensor("xT", [D, T + 2], F32, kind="ExternalInput").ap()
    xb_d = nc.dram_tensor("xb", [NB1, 128, KC * TB], F32, kind="ExternalInput").ap()
    xpb_d = nc.dram_tensor("xpb", [NB1, 128, KC * TB], F32, kind="ExternalInput").ap()
    win_d = nc.dram_tensor("w_in", [7, 128, KC * 512], F32, kind="ExternalInput").ap()
    wout_d = nc.dram_tensor("w_out", [KC, 128, KC * 128], F32, kind="ExternalInput").ap()
    wff1_d = nc.dram_tensor("w_ff1", [8, 128, KC * 512], F32, kind="ExternalInput").ap()
    wff2_d = nc.dram_tensor("w_ff2", [8, 128, 4 * D], F32, kind="ExternalInput").ap()
    prm_d = nc.dram_tensor("prm", [128, NPRM], F32, kind="ExternalInput").ap()
    cst_d = nc.dram_tensor("cst", [128, 640], F32, kind="ExternalInput").ap()
    outT_d = nc.dram_tensor("outT", [D, T], F32, kind="ExternalOutput").ap()

    xT_v = xT_d.rearrange("(kc p) t -> p kc t", p=128)
    k3 = lambda ap: ap.rearrange("p (k c) -> p k c", k=KC)
    outT_v = outT_d.rearrange("(j p) t -> p j t", p=128)

    with ExitStack() as es:
        S = Sched(nc, es)

        def sb(name, shape, dt, stack=es):
            return stack.enter_context(nc.sbuf_tensor("sb_" + name, shape, dt))

        def ps(name, shape, dt, stack=es):
            return stack.enter_context(nc.psum_tensor("ps_" + name, shape, dt))

        def act(out, in_, func, reads, writes, scale=1.0, bias=0.0):
            return S.op("act", lambda e: e.activation(out=out, in_=in_, func=func, scale=scale, bias=bias),
                        reads, writes)

        def tt(out, in0, in1, op, reads, writes, eng="dve"):
            return S.op(eng, lambda e: e.tensor_tensor(out=out, in0=in0, in1=in1, op=op), reads, writes)

        def ts(out, in0, s1, s2, op0, op1, reads, writes, eng="dve"):
            return S.op(eng, lambda e: e.tensor_scalar(out=out, in0=in0, scalar1=s1, scalar2=s2, op0=op0, op1=op1),
                        reads, writes)

        def stt(out, in0, scalar, in1, op0, op1, reads, writes, eng="dve"):
            return S.op(eng, lambda e: e.scalar_tensor_tensor(out=out, in0=in0, scalar=scalar, in1=in1,
                                                              op0=op0, op1=op1), reads, writes)

        def cp(out, in_, reads, writes, eng="dve"):
            return S.op(eng, lambda e: e.tensor_copy(out=out, in_=in_), reads, writes)

        def mm(out, lhsT, rhs, start, stop, reads, writes, signal, **kw):
            return S.op("pe", lambda e: e.matmul(out, lhsT=lhsT, rhs=rhs, start=start, stop=stop, **kw),
                        reads, writes, signal=signal)

        prm = sb("prm", [128, NPRM], F32)
        r_prm = Res("prm")
        ones_b = sb("ones_b", [128, 128], BF16)
        r_ones = Res("ones")
        S.op("dve", lambda e: e.memset(ones_b[:], 1.0), [], [r_ones])
        S.load("sp", prm[:], prm_d[:, :], r_prm)
        lbt = sb("lbt", [128, 12], F32)
        r_lb = Res("lb")
        tt(lbt[:, 8:12], prm[:, PL0:PL0 + 4], prm[:, PL1:PL1 + 4], ALU.subtract, [r_prm], [r_lb])
        act(lbt[:, 0:4], lbt[:, 8:12], AF.Exp, [r_lb], [r_lb], scale=-1.0)
        act(lbt[:, 0:4], lbt[:, 0:4], AF.Ln, [r_lb], [r_lb], scale=1.0, bias=1.0)
        act(lbt[:, 0:4], lbt[:, 0:4], AF.Exp, [r_lb], [r_lb], scale=-1.0)
        ts(lbt[:, 4:8], lbt[:, 0:4], -1.0, 1.0, ALU.mult, ALU.add, [r_lb], [r_lb])

        slab = [sb("slab%d" % i, [128, 4096], BF16) for i in range(16)]
        mixT = sb("mixT", [128, KC, T], BF16)
        r_mix = [[Res("mix_%d_%d" % (c, b)) for b in range(NB1)] for c in range(KC)]

        reg = [slab[r][:, :].rearrange("p (k c) -> p k c", k=KC) for r in range(8)]
        r_wff1 = [Res("wff1_%d" % i) for i in range(8)]
        W2SLAB = {0: 9, 1: 10, 2: 11, 3: 12, 4: 13, 5: 8, 6: 14, 7: 15}
        wff2c = [slab[W2SLAB[i]][:, :].rearrange("p (f c) -> p f c", f=4) for i in range(8)]
        r_wff2 = [Res("wff2_%d" % i) for i in range(8)]

        def f32view(si, j):
            return slab[si][:, 1024 * j:1024 * (j + 1)].bitcast(F32)
        with ExitStack() as p1:
            cst = sb("cst", [128, 640], F32, p1)
            r_cst = Res("cst")
            S.load("sp", cst[:], cst_d[:, :], r_cst)
            ident = sb("ident", [128, 128], BF16, p1)
            r_ident = Res("ident")
            cp(ident[:], cst[:, 0:128], [r_cst], [r_ident])
            mask4 = cst[:, 128:640]
            ones_f = sb("ones_f", [128, 128], F32, p1)
            r_onesf = Res("ones_f")
            S.op("dve", lambda e: e.memset(ones_f[:], 1.0), [], [r_onesf])
            win = reg[0:7]
            r_win = [Res("win%d" % i) for i in range(7)]
            r_win1h = [Res("win1_%d" % h) for h in range(H)]
            NXS = 3
            xs = [reg[7], slab[8][:, :].rearrange("p (k t) -> p k t", k=KC),
                  slab[9][:, :].rearrange("p (k t) -> p k t", k=KC)]
            r_xs = [Res("xs%d" % i) for i in range(NXS)]
            xh = sb("xh", [128, KC, 2], BF16, p1)
            r_xh = Res("xh")
            ones512 = sb("ones512", [128, TB], F32, p1)
            r_o512 = Res("ones512")
            S.op("dve", lambda e: e.memset(ones512[:], 1.0), [], [r_o512])
            NS = 2
            _sv = [f32view(10 + i // 4, i % 4) for i in range(12)]
            r_sv = [Res("sv%d" % i) for i in range(12)]

            class TSet:
                def __init__(self, own, h):
                    if own:
                        st = h % 2
                        ix = dict(kr=st, gg=2 + st, qq=4 + st, bb=6 + st, eq=8 + st, ek=10 + st)
                    else:
                        st = h % 3
                        ix = dict(kr=4 * st, gg=4 * st + 1, bb=4 * st + 2, ek=4 * st + 3)
                    for k_, i_ in ix.items():
                        setattr(self, k_, _sv[i_])
                        setattr(self, "r_" + k_, r_sv[i_])
            Qt = [[slab[13 + p][:, 512 * h:512 * (h + 1)] for h in range(H)] for p in range(2)]
            Kt = [[slab[13 + p][:, 2048 + 512 * h:2048 + 512 * (h + 1)] for h in range(H)] for p in range(2)]
            sm = [[sb("sm%d_%d" % (p, h), [128, 16], F32, p1) for h in range(H)] for p in range(2)]
            r_Qt = [[Res("Qt%d_%d" % (p, h)) for h in range(H)] for p in range(2)]
            r_Kt = [[Res("Kt%d_%d" % (p, h)) for h in range(H)] for p in range(2)]
            r_sm = [[Res("sm%d_%d" % (p, h)) for h in range(H)] for p in range(2)]
            for p in range(2):
                for h in range(H):
                    S.op("dve", lambda e, p=p, h=h: e.memset(sm[p][h][:], 0.0), [], [r_sm[p][h]])
            vtok = [slab[15][:, 2048 * p:2048 * (p + 1)].rearrange("p (g c) -> p g c", g=4) for p in range(2)]
            r_vtok = [[Res("vtok%d_%d" % (p, g)) for g in range(4)] for p in range(2)]
            sil2 = [[sb("sil%d_%d" % (q, h), [128, TB], F32, p1) for h in range(H)] for q in range(2)]
            r_sil2 = [[Res("sil%d_%d" % (q, h)) for h in range(H)] for q in range(2)]
            Ktok = [sb("Ktok%d" % i, [128, 512], BF16, p1) for i in range(2)]
            r_Ktok = [Res("Ktok%d" % i) for i in range(2)]
            scm = [sb("scm%d" % i, [128, 512], BF16, p1) for i in range(2)]
            r_scm = [Res("scm%d" % i) for i in range(2)]
            Sf = sb("Sf", [128, 512], F32, p1)
            Uf = sb("Uf", [128, 512], F32, p1)
            Sb = sb("Sb", [128, 512], BF16, p1)
            r_Sf, r_Uf, r_Sb = Res("Sf"), Res("Uf"), Res("Sb")
            S.op("dve", lambda e: e.memset(Sf[:], 0.0), [], [r_Sf])
            S.op("dve", lambda e: e.memset(Sb[:], 0.0), [], [r_Sb])
            og_ = sb("og", [128, 512], F32, p1)
            o2_ = sb("o2", [128, 512], BF16, p1)
            rs_ = sb("rs", [128, 512], F32, p1)
            r_og, r_o2, r_rs = Res("og"), Res("o2"), Res("rs")
            u_sb = sb("u_sb", [128, TB], F32, p1)
            z_ext = [sb("z_ext%d" % i, [128, TB + 2], F32, p1) for i in range(2)]
            acc = [sb("acc%d" % i, [128, TB], F32, p1) for i in range(2)]
            zc = sb("zc", [128, 4, 2], F32, p1)
            uh = sb("uh", [128, 2], F32, p1)
            r_u, r_uh = Res("u"), Res("uh")
            r_z = [Res("z0"), Res("z1")]
            r_acc = [Res("acc0"), Res("acc1")]
            r_zc = [Res("zc%d" % c) for c in range(4)]
            NPJ = 4
            PJ = [ps("PJ%d" % i, [128, 512], F32, p1) for i in range(NPJ)]
            r_PJ = [Res("PJ%d" % i) for i in range(NPJ)]
            TR = ps("TR", [128, 1024], BF16, p1)
            r_TR = Res("TR")
            SC = ps("SC", [128, 512], F32, p1)
            r_SC = Res("SC")
            SQ, r_SQ = SC, r_SC
            OO = ps("OO", [128, 512], F32, p1)
            r_OO = Res("OO")
            ST = ps("ST", [128, 512], F32, p1)
            r_ST = Res("ST")
            pj_i = [0]
            pj_n = [NPJ]

            def pj_alloc():
                free = [i for i in range(pj_n[0]) if r_PJ[i].w is None or r_PJ[i].rtick >= 0]
                assert free, "all in-proj PSUM banks hold data whose consumer has not been emitted yet"
                return min(free, key=lambda i: (r_PJ[i].rtick, r_PJ[i].wtick))

            def proj_fm(xsrc, r_x, col0, ncol=TB):
                i = pj_alloc()
                ch, off = col0 // 512, col0 % 512
                rw = r_win1h[off // 128] if ch == 1 else r_win[ch]
                for kc in range(KC):
                    mm(PJ[i][:, 0:ncol], win[ch][:, kc, off:off + 128], xsrc[:, kc, 0:ncol],
                       kc == 0, kc == KC - 1, [rw, r_x], [r_PJ[i]], kc == KC - 1)
                return PJ[i], r_PJ[i]

            seq = [(False, b) for b in range(NB1)] + [(True, b) for b in range(NB1)]
            NBLK = len(seq)

            def issue_x(n):
                own, b = seq[n]
                slot = n % NXS
                src = k3(xb_d[b]) if own else k3(xpb_d[b])
                S.load("pool", xs[slot][:], src, r_xs[slot])

            def gate_head(n, h):
                own = seq[n][0]
                p = n % 2
                t = TSet(own, h)
                smh, rsm = sm[p][h], r_sm[p][h]
                for c in range(4):
                    cs = slice(128 * c, 128 * (c + 1))
                    S.op("dve", lambda e, cs=cs: e.tensor_tensor_scan(out=t.bb[:, cs], data0=ones512[:, cs],
                                                                     data1=t.gg[:, cs], initial=0.0,
                                                                     op0=ALU.mult, op1=ALU.add),
                         [t.r_gg, r_o512], [t.r_bb])
                yield
                if own:
                    act(t.eq[:], t.bb[:], AF.Exp, [t.r_bb], [t.r_eq])
                act(t.ek[:], t.bb[:], AF.Exp, [t.r_bb], [t.r_ek], scale=-1.0)
                if not own:
                    act(smh[:, 12:16], t.bb[:, 127:512:128], AF.Exp, [t.r_bb], [rsm])
                yield
                if own:
                    cp(smh[:, 12:16], t.eq[:, 127:512:128], [t.r_eq], [rsm])
                    tt(Qt[p][h][:], t.qq[:], t.eq[:], ALU.mult, [t.r_qq, t.r_eq], [r_Qt[p][h]])
                    stt(Kt[p][h][:], t.kr[:], lbt[:, 4 + h:5 + h], t.ek[:], ALU.mult, ALU.mult,
                        [t.r_kr, r_lb, t.r_ek], [r_Kt[p][h]])
                else:
                    ts(smh[:, 8:12], smh[:, 12:16], lbt[:, 4 + h:5 + h], None, ALU.mult, ALU.bypass, [rsm, r_lb], [rsm])
                    for c in range(4):
                        cs = slice(128 * c, 128 * (c + 1))
                        stt(Kt[p][h][:, cs], t.kr[:, cs], smh[:, 8 + c:9 + c], t.ek[:, cs], ALU.mult, ALU.mult,
                            [t.r_kr, rsm, t.r_ek], [r_Kt[p][h]])
                yield

            def stage_ab(n):
                own = seq[n][0]
                p = n % 2
                xsrc, r_x = xs[n % NXS], r_xs[n % NXS]
                heads = {}

                def step():
                    for hh in list(heads):
                        try:
                            next(heads[hh])
                        except StopIteration:
                            del heads[hh]

                nset = 2 if own else 3
                for h in range(H):
                    t = TSet(own, h)
                    while (h - nset) in heads:
                        step()
                        yield
                    bank, rb = proj_fm(xsrc, r_x, CF + 128 * h)
                    act(t.gg[:], bank[:], AF.Exp, [rb], [t.r_gg], scale=-1.0)
                    act(t.gg[:], t.gg[:], AF.Ln, [t.r_gg], [t.r_gg], scale=1.0, bias=1.0)
                    act(t.gg[:], t.gg[:], AF.Exp, [t.r_gg], [t.r_gg], scale=-1.0)
                    act(t.kr[:], t.gg[:], AF.Identity, [t.r_gg], [t.r_kr], scale=-1.0, bias=1.0)
                    act(t.gg[:], t.gg[:], AF.Ln, [t.r_gg, r_lb], [t.r_gg],
                        scale=lbt[:, 4 + h:5 + h], bias=lbt[:, h:h + 1])
                    if own:
                        step()
                        yield
                    if own:
                        bank, rb = proj_fm(xsrc, r_x, CQ + 128 * h)
                        act(t.qq[:], bank[:], AF.Copy, [rb], [t.r_qq])
                        step()
                        yield
                    heads[h] = gate_head(n, h)
                    g = h
                    i = pj_alloc()
                    for kc in range(KC):
                        mm(PJ[i][:], xsrc[:, kc, 128 * g:128 * (g + 1)], win[2][:, kc, :],
                           kc == 0, kc == KC - 1, [r_win[2], r_x], [r_PJ[i]], kc == KC - 1)
                    act(vtok[p][:, g, :], PJ[i][:], AF.Copy, [r_PJ[i]], [r_vtok[p][g]])
                    step()
                    yield
                while heads:
                    step()
                    yield

            def stage_c_prev(n):
                p = n % 2
                HS = [slice(128 * h, 128 * (h + 1)) for h in range(H)]
                ktk = [Ktok[0], Ktok[1], scm[0], scm[1]]
                r_ktk = [r_Ktok[0], r_Ktok[1], r_scm[0], r_scm[1]]
                stb = [ST, OO, SC, ST]
                r_stb = [r_ST, r_OO, r_SC, r_ST]

                def tr2(g0):
                    for g in (g0, g0 + 1):
                        gs = slice(128 * g, 128 * (g + 1))
                        for h in range(H):
                            o0 = 512 * (g - g0) + 128 * h
                            S.op("pe", lambda e, h=h, gs=gs, o0=o0: e.transpose(TR[:, o0:o0 + 128], Kt[p][h][:, gs], ident[:]),
                                 [r_Kt[p][h], r_ident], [r_TR], signal=(h == H - 1 and g == g0 + 1))

                def cp2(g0):
                    for g in (g0, g0 + 1):
                        cp(ktk[g][:], TR[:, 512 * (g - g0):512 * (g - g0 + 1)], [r_TR], [r_ktk[g]])

                def st(g):
                    for h in range(H):
                        mm(stb[g][:, HS[h]], ktk[g][:, HS[h]], vtok[p][:, g, HS[h]], True, True,
                           [r_ktk[g], r_vtok[p][g]], [r_stb[g]], h == H - 1)

                def upd(g):
                    for h in range(H):
                        stt(Sf[:, HS[h]], Sf[:, HS[h]], sm[p][h][:, 12 + g:13 + g], stb[g][:, HS[h]], ALU.mult, ALU.add,
                            [r_stb[g], r_Sf, r_sm[p][h]], [r_Sf])

                tr2(0)
                yield
                cp2(0)
                yield
                tr2(2)
                st(0)
                st(1)
                yield
                cp2(2)
                upd(0)
                yield
                st(2)
                st(3)
                upd(1)
                yield
                upd(2)
                yield
                upd(3)
                act(Sb[:], Sf[:], AF.Copy, [r_Sf], [r_Sb])
                yield

            def stage_c_pipe(n, SQ, r_SQ):
                own, blk = seq[n]
                p = n % 2

                def grp(g):
                    gs = slice(128 * g, 128 * (g + 1))
                    kb = g % 2
                    HS = [slice(128 * h, 128 * (h + 1)) for h in range(H)]
                    for h in range(H):
                        S.op("pe", lambda e, h=h: e.transpose(TR[:, HS[h]], Kt[p][h][:, gs], ident[:]),
                             [r_Kt[p][h], r_ident], [r_TR], signal=(h == H - 1))
                    if own:
                        for h in range(H):
                            mm(SC[:, HS[h]], Kt[p][h][:, gs], Qt[p][h][:, gs], True, True,
                               [r_Kt[p][h], r_Qt[p][h]], [r_SC], h == H - 1)
                    yield
                    cp(Ktok[kb][:], TR[:, 0:512], [r_TR], [r_Ktok[kb]])
                    if own:
                        tt(scm[kb][:], SC[:], mask4, ALU.mult, [r_SC, r_cst], [r_scm[kb]])
                    yield
                    if own:
                        for h in range(H):
                            mm(OO[:, HS[h]], vtok[p][:, g, HS[h]], scm[kb][:, HS[h]], True, False,
                               [r_vtok[p][g], r_scm[kb]], [r_OO], False)
                            mm(OO[:, HS[h]], Sb[:, HS[h]], Qt[p][h][:, gs], False, True, [r_Sb, r_Qt[p][h]], [r_OO], h == H - 1)
                    for h in range(H):
                        mm(ST[:, HS[h]], Ktok[kb][:, HS[h]], vtok[p][:, g, HS[h]], True, True,
                           [r_Ktok[kb], r_vtok[p][g]], [r_ST], h == H - 1)
                    yield
                    if own:
                        tt(Uf[:], ST[:], Sf[:], ALU.add, [r_ST, r_Sf], [r_Uf])
                        act(og_[:], OO[:], AF.Copy, [r_OO], [r_og])
                        act(o2_[:], OO[:], AF.Square, [r_OO], [r_o2])
                    else:
                        for h in range(H):
                            stt(Sf[:, HS[h]], Sf[:, HS[h]], sm[p][h][:, 12 + g:13 + g], ST[:, HS[h]], ALU.mult, ALU.add,
                                [r_ST, r_Sf, r_sm[p][h]], [r_Sf])
                        if g == 3:
                            act(Sb[:], Sf[:], AF.Copy, [r_Sf], [r_Sb])
                        return
                    yield
                    for h in range(H):
                        dcol = sm[p][h][:, 12 + g:13 + g]
                        ts(Sf[:, HS[h]], Uf[:, HS[h]], dcol, None, ALU.mult, ALU.bypass, [r_Uf, r_sm[p][h]], [r_Sf])
                        act(Sb[:, HS[h]], Uf[:, HS[h]], AF.Copy, [r_Uf, r_sm[p][h]], [r_Sb], scale=dcol)
                    mm(SQ[:], ones_b[:], o2_[:], True, True, [r_ones, r_o2], [r_SQ], True)
                    yield
                    act(rs_[:], SQ[:], AF.Ln, [r_SQ], [r_rs], scale=1.0 / 128.0, bias=EPS)
                    act(rs_[:], rs_[:], AF.Exp, [r_rs], [r_rs], scale=-0.5)
                    yield
                    tt(og_[:], og_[:], rs_[:], ALU.mult, [r_og, r_rs], [r_og])
                    for h in range(H):
                        stt(mixT[:, h, blk * TB + 128 * g: blk * TB + 128 * (g + 1)], sil2[p][h][:, gs],
                            prm[:, PGN + h:PGN + h + 1], og_[:, HS[h]], ALU.mult, ALU.mult,
                            [r_sil2[p][h], r_prm, r_og], [r_mix[h][blk]])
                    yield

                lag = 3 if own else 2
                gens = {}
                rnd = -2
                started = 0
                while started < 4 or gens:
                    if started < 4 and rnd >= started * lag - 2:
                        gens[started] = grp(started)
                        started += 1
                    for g in sorted(gens):
                        try:
                            next(gens[g])
                        except StopIteration:
                            del gens[g]
                    rnd += 1
                    yield

            def stage_c(n):
                own, blk = seq[n]
                p = n % 2

                def pre(g):
                    gs = slice(128 * g, 128 * (g + 1))
                    kb = g % 2
                    for h in range(H):
                        hs = slice(128 * h, 128 * (h + 1))
                        S.op("pe", lambda e, h=h, hs=hs: e.transpose(TR[:, hs], Kt[p][h][:, gs], ident[:]),
                             [r_Kt[p][h], r_ident], [r_TR], signal=(h == H - 1))
                    if own:
                        for h in range(H):
                            hs = slice(128 * h, 128 * (h + 1))
                            mm(SC[:, hs], Kt[p][h][:, gs], Qt[p][h][:, gs], True, True,
                               [r_Kt[p][h], r_Qt[p][h]], [r_SC], h == H - 1)
                    yield
                    cp(Ktok[kb][:], TR[:, 0:512], [r_TR], [r_Ktok[kb]])
                    if own:
                        tt(scm[kb][:], SC[:], mask4, ALU.mult, [r_SC, r_cst], [r_scm[kb]])
                    yield

                def main(g):
                    gs = slice(128 * g, 128 * (g + 1))
                    kb = g % 2
                    if own:
                        for h in range(H):
                            hs = slice(128 * h, 128 * (h + 1))
                            mm(OO[:, hs], vtok[p][:, g, hs], scm[kb][:, hs], True, False,
                               [r_vtok[p][g], r_scm[kb]], [r_OO], False)
                            mm(OO[:, hs], Sb[:, hs], Qt[p][h][:, gs], False, True, [r_Sb, r_Qt[p][h]], [r_OO], h == H - 1)
                    for h in range(H):
                        hs = slice(128 * h, 128 * (h + 1))
                        mm(ST[:, hs], Ktok[kb][:, hs], vtok[p][:, g, hs], True, True,
                           [r_Ktok[kb], r_vtok[p][g]], [r_ST], h == H - 1)
                    yield
                    if own:
                        tt(Uf[:], ST[:], Sf[:], ALU.add, [r_ST, r_Sf], [r_Uf])
                        act(og_[:], OO[:], AF.Copy, [r_OO], [r_og])
                        act(o2_[:], OO[:], AF.Square, [r_OO], [r_o2])
                    else:
                        for h in range(H):
                            hs = slice(128 * h, 128 * (h + 1))
                            stt(Sf[:, hs], Sf[:, hs], sm[p][h][:, 12 + g:13 + g], ST[:, hs], ALU.mult, ALU.add,
                                [r_ST, r_Sf, r_sm[p][h]], [r_Sf])
                    yield
                    if own:
                        for h in range(H):
                            hs = slice(128 * h, 128 * (h + 1))
                            dcol = sm[p][h][:, 12 + g:13 + g]
                            ts(Sf[:, hs], Uf[:, hs], dcol, None, ALU.mult, ALU.bypass, [r_Uf, r_sm[p][h]], [r_Sf])
                            act(Sb[:, hs], Uf[:, hs], AF.Copy, [r_Uf, r_sm[p][h]], [r_Sb], scale=dcol)
                    else:
                        act(Sb[:], Sf[:], AF.Copy, [r_Sf], [r_Sb])
                    if own:
                        mm(SQ[:], ones_b[:], o2_[:], True, True, [r_ones, r_o2], [r_SQ], True)
                    yield
                    if own:
                        act(rs_[:], SQ[:], AF.Ln, [r_SQ], [r_rs], scale=1.0 / 128.0, bias=EPS)
                        act(rs_[:], rs_[:], AF.Exp, [r_rs], [r_rs], scale=-0.5)
                        yield
                        tt(og_[:], og_[:], rs_[:], ALU.mult, [r_og, r_rs], [r_og])
                        for h in range(H):
                            hs = slice(128 * h, 128 * (h + 1))
                            stt(mixT[:, h, blk * TB + 128 * g: blk * TB + 128 * (g + 1)], sil2[p][h][:, gs],
                                prm[:, PGN + h:PGN + h + 1], og_[:, hs], ALU.mult, ALU.mult,
                                [r_sil2[p][h], r_prm, r_og], [r_mix[h][blk]])
                        yield

                yield from pre(0)
                for g in range(4):
                    gm = main(g)
                    gp = pre(g + 1) if g + 1 < 4 else None
                    alive = True
                    while alive:
                        alive = False
                        try:
                            next(gm)
                            alive = True
                        except StopIteration:
                            pass
                        if gp is not None:
                            try:
                                next(gp)
                                alive = True
                            except StopIteration:
                                gp = None
                        if alive:
                            yield

            def stage_og(n):
                xsrc, r_x = xs[n % NXS], r_xs[n % NXS]
                for h in range(H):
                    bank, rb = proj_fm(xsrc, r_x, COG + 128 * h)
                    sl, rsl = sil2[n % 2][h], r_sil2[n % 2][h]
                    act(sl[:], bank[:], AF.Exp, [rb], [rsl], scale=-1.0)
                    act(sl[:], sl[:], AF.Ln, [rsl], [rsl], scale=1.0, bias=1.0)
                    act(sl[:], sl[:], AF.Exp, [rsl], [rsl], scale=-1.0)
                    yield
                    tt(sl[:], bank[:], sl[:], ALU.mult, [rb, rsl], [rsl])
                    yield

            def stage_d(n, cts=(0, 1, 2, 3), delay=0):
                own, blk = seq[n]
                xsrc, r_x = xs[n % NXS], r_xs[n % NXS]
                for _ in range(delay):
                    yield
                for ct in cts:
                    zi = ct % 2
                    cw = lambda j: prm[:, PCW + 4 * j + ct:PCW + 4 * j + ct + 1]
                    if blk == 0:
                        bank, rb = proj_fm(xh, r_xh, CU + 128 * ct, ncol=2)
                        cp(uh[:], bank[:, 0:2], [rb], [r_uh])
                        yield
                        bank, rb = proj_fm(xh, r_xh, CC + 128 * ct, ncol=2)
                        tt(zc[:, ct, :], bank[:, 0:2], uh[:], ALU.mult, [rb, r_uh], [r_zc[ct]])
                        yield
                    bank, rb = proj_fm(xsrc, r_x, CU + 128 * ct)
                    act(u_sb[:], bank[:], AF.Copy, [rb], [r_u])
                    yield
                    bank, rb = proj_fm(xsrc, r_x, CC + 128 * ct)
                    yield
                    tt(z_ext[zi][:, 2:TB + 2], bank[:], u_sb[:], ALU.mult, [rb, r_u], [r_z[zi]])
                    cp(z_ext[zi][:, 0:2], zc[:, ct, :], [r_zc[ct]], [r_z[zi]])
                    cp(zc[:, ct, :], z_ext[zi][:, TB:TB + 2], [r_z[zi]], [r_zc[ct]])
                    yield
                    ts(acc[zi][:], z_ext[zi][:, 2:TB + 2], cw(2), None, ALU.mult, ALU.bypass,
                       [r_z[zi], r_prm], [r_acc[zi]])
                    yield
                    stt(acc[zi][:], z_ext[zi][:, 1:TB + 1], cw(1), acc[zi][:], ALU.mult, ALU.add,
                        [r_z[zi], r_prm, r_acc[zi]], [r_acc[zi]])
                    stt(acc[zi][:], z_ext[zi][:, 0:TB], cw(0), acc[zi][:], ALU.mult, ALU.add,
                        [r_z[zi], r_prm, r_acc[zi]], [r_acc[zi]])
                    yield
                    bank, rb = proj_fm(xsrc, r_x, CB + 128 * ct)
                    yield
                    tt(mixT[:, 4 + ct, blk * TB:(blk + 1) * TB], bank[:], acc[zi][:], ALU.mult, [rb, r_acc[zi]],
                       [r_mix[4 + ct][blk]])
                    yield

            issue_x(0)
            for h in range(H):
                S.load("pool", win[1][:, :, 128 * h:128 * (h + 1)], k3(win_d[1])[:, :, 128 * h:128 * (h + 1)], r_win1h[h])
            S.load("pool", win[2][:], k3(win_d[2]), r_win[2])
            issue_x(1)
            S.load("pool", xh[:], xT_v[:, :, 0:2], r_xh)
            for i in (0, 3, 4, 5, 6):
                S.load("pool", win[i][:], k3(win_d[i]), r_win[i])

            def drive(gens, periods=None):
                periods = periods or [1] * len(gens)
                live = [[g, pd] for g, pd in zip(gens, periods) if g is not None]
                rnd = 0
                while live:
                    for it in list(live):
                        if rnd % it[1] == 0 or len(live) == 1:
                            try:
                                next(it[0])
                            except StopIteration:
                                live.remove(it)
                    rnd += 1

            def wff1_load(i):
                S.load("pool", reg[i][:], k3(wff1_d[i]), r_wff1[i])

            def wff2_load(i):
                S.load("pool", wff2c[i][:], wff2_d[i].rearrange("p (f c) -> p f c", f=4), r_wff2[i])

            drive([stage_ab(0)])
            for n in range(NBLK):
                if n + 2 < NBLK:
                    issue_x(n + 2)
                own = seq[n][0]
                g_ab = stage_ab(n + 1) if n + 1 < NBLK else None

                if not own:
                    g_c = stage_c_prev(n)
                elif n == NBLK - 1:
                    pj_n[0] = NPJ - 1
                    g_c = stage_c_pipe(n, PJ[NPJ - 1], r_PJ[NPJ - 1])
                else:
                    g_c = stage_c(n)
                c_done = [False]

                def c_wrap():
                    yield from g_c
                    c_done[0] = True

                def ab_then_og(n=n):
                    if g_ab is not None:
                        yield from g_ab
                    if n + 1 < NBLK and seq[n + 1][0]:
                        yield from stage_og(n + 1)

                if own:
                    drive([ab_then_og(), c_wrap(), stage_d(n, (0, 2)), stage_d(n, (1, 3), 3)], [1, 1, 1, 1])
                else:
                    drive([ab_then_og(), c_wrap()], [1, 1])
                if n == NBLK - 3:
                    r_wff2[0].r = {k: S.cnt[k] for k in ("pe", "act", "dve")}
                    wff2_load(0)
                if n == NBLK - 2:
                    fence6 = {k: S.cnt[k] for k in ("pe", "act", "dve")}
                    for kind, i in (("1", 0), ("1", 1), ("2", 1), ("1", 2), ("2", 2), ("1", 3), ("2", 3)):
                        if kind == "1":
                            r_wff1[i].r = dict(fence6)
                            wff1_load(i)
                        else:
                            r_wff2[i].r = dict(fence6)
                            wff2_load(i)
            fence7 = {k: S.cnt[k] for k in ("pe", "act", "dve")}
            lazy = []
            lazy_left = {}
            for kind, i in (("1", 4), ("2", 4), ("1", 5), ("2", 5), ("1", 6), ("2", 6), ("1", 7), ("2", 7)):
                (r_wff1 if kind == "1" else r_wff2)[i].r = dict(fence7)
                lazy_left[(kind, i)] = 1
                lazy.append((kind, i, 0))

            def pump(k=1):
                for _ in range(k):
                    if not lazy:
                        return
                    kind, i, q = lazy.pop(0)
                    if kind == "1":
                        wff1_load(i)
                    else:
                        wff2_load(i)
                    lazy_left[(kind, i)] -= 1

            def ensure(kind, i):
                while lazy_left.get((kind, i), 0) > 0:
                    pump(1)

        with ExitStack() as p2:
            fence = {k: S.cnt[k] for k in ("pe", "act", "dve") if S.cnt[k] > 0}

            def Res2(name):
                r = Res(name)
                r.r = dict(fence)
                return r

            NWO = KC
            wo = [sb("wo%d" % i, [128, KC, 128], BF16, p2) for i in range(NWO)]
            r_wo = [Res2("wo%d" % i) for i in range(NWO)]
            hA = [sb("hA%d" % i, [128, KC, T2], F32, p2) for i in range(2)]
            r_hA = [[Res2("hA%d_%d" % (i, j)) for j in range(KC)] for i in range(2)]
            h1b = [sb("h1b%d" % i, [128, KC, T2], BF16, p2) for i in range(2)]
            r_h1b = [[Res2("h1b%d_%d" % (i, j)) for j in range(KC)] for i in range(2)]
            NSQ = 2
            hsq = [sb("hsq%d" % i, [128, T2], BF16, p2) for i in range(NSQ)]
            r_hsq = [Res2("hsq%d" % i) for i in range(NSQ)]
            hbc = [sb("hbc%d" % i, [128, T2], BF16, p2) for i in range(NSQ)]
            r_hbc = [Res2("hbc%d" % i) for i in range(NSQ)]
            mean = sb("mean", [128, T2], F32, p2)
            rstd = sb("rstd", [128, T2], F32, p2)
            nmr = sb("nmr", [128, T2], F32, p2)
            r_mean, r_rstd, r_nmr = Res2("mean"), Res2("rstd"), Res2("nmr")
            NHID = 4
            hid = [sb("hid%d" % i, [128, T2], BF16, p2) for i in range(NHID)]
            r_hid = [Res2("hid%d" % i) for i in range(NHID)]
            OP = ps("OP", [128, 512], F32, p2)
            r_OP = Res2("OP")
            SS = ps("SS", [128, 512], F32, p2)
            r_SS = Res2("SS")
            F1 = [ps("F1%d" % i, [128, 512], F32, p2) for i in range(2)]
            r_F1 = [Res2("F10"), Res2("F11")]
            AC = [ps("AC%d" % i, [128, 512], F32, p2) for i in range(4)]
            r_AC = [Res2("AC%d" % j) for j in range(4)]

            wo_n = [0]
            wo_loaded = [False] * KC

            OPb = [OP, SS]
            r_OPb = [r_OP, r_SS]

            def ln_stats(buf, rbuf, cast_on_dve=False):
                for k in range(KC + 2):
                    if k >= 2:
                        j = k - 2
                        q = j % NSQ
                        mm(SS[:, 0:T2], ones_b[:], hbc[q][:], j == 0, j == KC - 1,
                           [r_ones, r_hbc[q]], [r_SS], True, skip_group_check=True)
                        mm(SS[:, T2:2 * T2], ones_b[:], hsq[q][:], False, j == KC - 1,
                           [r_ones, r_hsq[q]], [r_SS], True, skip_group_check=True)
                    if k < KC:
                        q = k % NSQ
                        if cast_on_dve:
                            cp(hbc[q][:], hA[buf][:, k, :], [rbuf[k]], [r_hbc[q]])
                        else:
                            act(hbc[q][:], hA[buf][:, k, :], AF.Copy, [rbuf[k]], [r_hbc[q]])
                        act(hsq[q][:], hA[buf][:, k, :], AF.Square, [rbuf[k]], [r_hsq[q]])
                    yield
                ts(mean[:], SS[:, 0:T2], 1.0 / D, None, ALU.mult, ALU.bypass, [r_SS], [r_mean])
                tt(nmr[:], mean[:], mean[:], ALU.mult, [r_mean], [r_nmr])
                stt(rstd[:], SS[:, T2:2 * T2], 1.0 / D, nmr[:], ALU.mult, ALU.subtract, [r_SS, r_nmr], [r_rstd])
                yield
                act(rstd[:], rstd[:], AF.Ln, [r_rstd], [r_rstd], scale=1.0, bias=EPS)
                act(rstd[:], rstd[:], AF.Exp, [r_rstd], [r_rstd], scale=-0.5)
                yield
                stt(nmr[:], mean[:], -1.0, rstd[:], ALU.mult, ALU.mult, [r_mean, r_rstd], [r_nmr])
                yield

            N_STATS = KC + 5

            def normalize(buf, rh, pg, pb, post, on_act=False):
                for k in range(KC + 1):
                    if k < KC:
                        tt(hA[buf][:, k, :], hA[buf][:, k, :], rstd[:], ALU.mult, [rh[k], r_rstd], [rh[k]])
                    yield
                    if k < KC:
                        tt(hA[buf][:, k, :], hA[buf][:, k, :], nmr[:], ALU.add, [rh[k], r_nmr], [rh[k]])
                        if on_act:
                            act(hA[buf][:, k, :], hA[buf][:, k, :], AF.Identity, [rh[k], r_prm], [rh[k]],
                                scale=prm[:, pg + k:pg + k + 1], bias=prm[:, pb + k:pb + k + 1])
                        else:
                            ts(hA[buf][:, k, :], hA[buf][:, k, :], prm[:, pg + k:pg + k + 1], prm[:, pb + k:pb + k + 1],
                               ALU.mult, ALU.add, [rh[k], r_prm], [rh[k]])
                    if k >= 1:
                        post(k - 1)
                    yield

            N_NORM = 2 * (KC + 1)

            def prologue(b):
                buf = b % 2
                tok = slice(b * T2, (b + 1) * T2)
                rh = r_hA[buf]
                wsl = {}

                def wo_load(j):
                    wsl[j] = j
                    if not wo_loaded[j]:
                        wo_loaded[j] = True
                        S.load("pool", wo[j][:], wout_d[j].rearrange("p (k c) -> p k c", k=KC), r_wo[j])

                for j in range(NWO):
                    wo_load(j)
                yield
                for j in range(KC):
                    S.load("sp", hA[buf][:, j, :], xT_v[:, j, 2 + b * T2:2 + (b + 1) * T2], rh[j])
                yield
                for k in range(KC + 1):
                    if k >= 1 and k + NWO - 1 < KC:
                        wo_load(k + NWO - 1)
                    if k < KC:
                        w = wsl[k]
                        for c in range(KC):
                            mm(OPb[k % 2][:, 0:T2], wo[w][:, c, :], mixT[:, c, tok], c == 0, c == KC - 1,
                               [r_wo[w], r_mix[c][b // 2]], [r_OPb[k % 2]], c == KC - 1)
                    if k >= 1:
                        j = k - 1
                        stt(hA[buf][:, j, :], hA[buf][:, j, :], ALPHA, OPb[j % 2][:, 0:T2], ALU.mult, ALU.add,
                            [r_OPb[j % 2], rh[j]], [rh[j]])
                    yield
                yield from ln_stats(buf, rh)

                def post(j):
                    act(h1b[buf][:, j, :], hA[buf][:, j, :], AF.Copy, [rh[j]], [r_h1b[buf][j]])

                yield from normalize(buf, rh, PG1, PB1, post)

            N_PRO = 2 + (KC + 1) + N_STATS + N_NORM

            def ffn(b):
                buf = b % 2

                def ffn1(f):
                    half = f % 2
                    ensure("1", f // 4)
                    for c in range(KC):
                        mm(F1[half][:, 0:T2], reg[f // 4][:, c, 128 * (f % 4):128 * (f % 4 + 1)], h1b[buf][:, c, :],
                           c == 0, c == KC - 1, [r_wff1[f // 4], r_h1b[buf][c]], [r_F1[half]], c == KC - 1)
                    act(F1[half][:, 0:T2], F1[half][:, 0:T2], AF.Relu, [r_F1[half]], [r_F1[half]])
                    act(hid[f % NHID][:], F1[half][:, 0:T2], AF.Square, [r_F1[half]], [r_hid[f % NHID]])

                def ffn2(f):
                    ensure("2", f // 4)
                    for j in range(KC):
                        half = j % 2
                        mm(AC[j // 2][:, half * T2:(half + 1) * T2], wff2c[f // 4][:, f % 4, 128 * j:128 * (j + 1)], hid[f % NHID][:],
                           (f == 0 and half == 0), f == NF - 1, [r_wff2[f // 4], r_hid[f % NHID]], [r_AC[j // 2]],
                           (f == NF - 1) or (j == KC - 1), skip_group_check=True)

                ffn1(0)
                ffn1(1)
                for f in range(NF):
                    ffn2(f)
                    if f + 2 < NF:
                        ffn1(f + 2)
                    yield

            def resid2(b):
                buf = b % 2
                rh = r_hA[buf]
                for j in range(KC):
                    half = j % 2
                    stt(hA[buf][:, j, :], hA[buf][:, j, :], ALPHA, AC[j // 2][:, half * T2:(half + 1) * T2],
                        ALU.mult, ALU.add, [r_AC[j // 2], rh[j]], [rh[j]])

            def epilogue(b):
                buf = b % 2
                tok = slice(b * T2, (b + 1) * T2)
                rh = r_hA[buf]
                yield from ln_stats(buf, rh, cast_on_dve=(b == NB2 - 1))

                def post(j):
                    S.store("sp", outT_v[:, j, tok], hA[buf][:, j, :], rh[j])

                yield from normalize(buf, rh, PG2, PB2, post, on_act=(b == NB2 - 1))

            N_EPI = N_STATS + N_NORM

            def run(gen, n=None):
                if gen is None:
                    return False
                k = 0
                while n is None or k < n:
                    try:
                        next(gen)
                    except StopIteration:
                        return False
                    k += 1
                return True

            g_p0 = prologue(0)
            run(g_p0, 1)
            pump(len(lazy))
            run(g_p0)
            for b in range(NB2):
                g_f = ffn(b)
                chain = []
                n_steps = 0
                if b > 0:
                    chain.append(epilogue(b - 1))
                    n_steps += N_EPI
                if b + 1 < NB2:
                    chain.append(prologue(b + 1))
                    n_steps += N_PRO
                done = 0
                if b + 1 < NB2:
                    g_pro = chain[-1]
                    run(g_pro, 1)
                    done += 1
                for f in range(NF):
                    run(g_f, 1)
                    pump(1)
                    target = min(n_steps, -(-(f + 1) * n_steps // (NF - 2)))
                    while done < target and chain:
                        if run(chain[0], 1):
                            done += 1
                        else:
                            chain.pop(0)
                run(g_f)
                for g_ in chain:
                    run(g_)
                resid2(b)
            run(epilogue(NB2 - 1))
            S.finish("sp")
    return nc


def _layout_inputs(x, w_in, lb_logits, gate_norm_w, conv_w, w_out, ln1_g, ln1_b, w_ff1, w_ff2, ln2_g, ln2_b):
    f = lambda a: np.ascontiguousarray(np.asarray(a, dtype=np.float32))
    x = f(x)
    fm4 = lambda v: f(v).reshape(4, 128).T
    fm8 = lambda v: f(v).reshape(8, 128).T
    prm = np.zeros((128, NPRM), np.float32)
    prm[:, PL0:PL0 + 4] = fm4(lb_logits[0])
    prm[:, PL1:PL1 + 4] = fm4(lb_logits[1])
    prm[:, PGN:PGN + 4] = fm4(gate_norm_w[0])
    for j in range(3):
        prm[:, PCW + 4 * j:PCW + 4 * j + 4] = fm4(conv_w[0, j])
    prm[:, PG1:PG1 + 8] = fm8(ln1_g[0])
    prm[:, PB1:PB1 + 8] = fm8(ln1_b[0])
    prm[:, PG2:PG2 + 8] = fm8(ln2_g[0])
    prm[:, PB2:PB2 + 8] = fm8(ln2_b[0])
    cst = np.zeros((128, 640), np.float32)
    cst[:, 0:128] = np.eye(128, dtype=np.float32)
    tri = np.triu(np.ones((128, 128), np.float32))
    cst[:, 128:640] = np.tile(tri, (1, 4))
    wo_l = f(f(w_out[0]).reshape(KC, 128, KC, 128).transpose(2, 1, 0, 3).reshape(KC, 128, KC * 128))
    chunked = lambda w, n: f(f(w).reshape(KC, 128, n, 512).transpose(2, 1, 0, 3).reshape(n, 128, KC * 512))
    w2_l = f(f(w_ff2[0]).reshape(8, 4, 128, D).transpose(0, 2, 1, 3).reshape(8, 128, 4 * D))
    shared = {"w_in": chunked(w_in[0], 7), "w_out": wo_l, "w_ff1": chunked(w_ff1[0], 8), "w_ff2": w2_l,
              "prm": prm, "cst": cst}
    in_maps = []
    for c in range(N_CORES):
        b, half = c // 2, c % 2
        xT = np.zeros((D, T + 2), np.float32)
        xpT = np.zeros((D, T), np.float32)
        xT[:, 2:] = x[b, half * T:(half + 1) * T, :].T
        if half == 1:
            xT[:, 0:2] = x[b, T - 2:T, :].T
            xpT[:, :] = x[b, 0:T, :].T
        blocked = lambda a: f(a.reshape(KC, 128, NB1, TB).transpose(2, 1, 0, 3).reshape(NB1, 128, KC * TB))
        m = dict(shared)
        m["xT"] = xT
        m["xb"] = blocked(xT[:, 2:])
        m["xpb"] = blocked(xpT)
        in_maps.append(m)
    return in_maps


_NC_CACHE = {}


def kernel(x, w_in, lb_logits, gate_norm_w, conv_w, w_out, ln1_g, ln1_b, w_ff1, w_ff2, ln2_g, ln2_b):
    in_maps = _layout_inputs(x, w_in, lb_logits, gate_norm_w, conv_w, w_out, ln1_g, ln1_b,
                             w_ff1, w_ff2, ln2_g, ln2_b)
    if "nc" not in _NC_CACHE:
        _NC_CACHE["nc"] = build_program()
    res = run_bass_kernel_spmd(_NC_CACHE["nc"], in_maps, core_ids=list(range(N_CORES)))
    out = np.empty((4, 2 * T, D), np.float32)
    for c in range(N_CORES):
        b, half = c // 2, c % 2
        out[b, half * T:(half + 1) * T, :] = np.asarray(res.results[c]["outT"]).T
    return out
```
